# Optimizing a Trainium2 kernel written in Bass

```python
import jax, jax.numpy as jnp
from jax import lax
import numpy as np

D_MODEL = 1024
BATCH = 32
SEQ = 2048
DEPTH = 1

GLA_HEADS = 4
GLA_V_W = D_MODEL // 2
GLA_DV = GLA_V_W // GLA_HEADS
GLA_DK = GLA_DV // 2
GLA_QK_W = GLA_HEADS * GLA_DK
GLA_GATE_RANK = 16
GLA_LOGIT_NORM = 16.0
GLA_CHUNK = 64
RWKV_HEAD = 64
RWKV_W = D_MODEL // 2
RWKV_HEADS = RWKV_W // RWKV_HEAD
RWKV_DECAY_LORA = 64
RWKV_AAA_LORA = 64
RWKV_GATE_LORA = 128
RWKV_GN_EPS = RWKV_HEAD * 1e-5
GLA_SPLITS = (GLA_QK_W, GLA_QK_W, GLA_V_W, GLA_V_W, GLA_GATE_RANK, GLA_GATE_RANK)
RWKV_SPLITS = (RWKV_W, RWKV_W, RWKV_W, RWKV_DECAY_LORA, RWKV_AAA_LORA, RWKV_GATE_LORA)
GLA_PROJ_W = sum(GLA_SPLITS)
RWKV_PROJ_W = sum(RWKV_SPLITS)
GATE_PROJ_W = 2 * D_MODEL
N_PROJ = GLA_PROJ_W + RWKV_PROJ_W + GATE_PROJ_W
D_FF = ((8 * D_MODEL // 3) + 63) // 64 * 64
CONV_W = 3
NORM_EPS = 1e-6
HEAD_NORM_EPS = 1e-5

kernel_name = "bidir_gla_rwkv7_gated_hybrid"


def _rmsnorm(x, g):
    xf = x.astype(jnp.float32)
    y = xf * lax.rsqrt(jnp.mean(xf * xf, axis=-1, keepdims=True) + NORM_EPS)
    return (y * g.astype(jnp.float32)).astype(x.dtype)


def _shift_prev(u):
    return jnp.pad(u[:, :-1], ((0, 0), (1, 0), (0, 0)))


def _shift_next(u):
    return jnp.pad(u[:, 1:], ((0, 0), (0, 1), (0, 0)))


def _split(t, sizes):
    return jnp.split(t, np.cumsum(sizes)[:-1].tolist(), axis=-1)


def _gla_chunked(q, k, v, log_a):
    f32 = jnp.float32
    B_, T, H, K = q.shape
    V = v.shape[-1]
    C = GLA_CHUNK
    n = T // C
    q = q.astype(f32).reshape(B_, n, C, H, K)
    k = k.astype(f32).reshape(B_, n, C, H, K)
    v = v.astype(f32).reshape(B_, n, C, H, V)
    b = jnp.cumsum(log_a.astype(f32).reshape(B_, n, C, H, K), axis=2)
    b_ref = b[:, :, C // 2:C // 2 + 1]
    qi = q * jnp.exp(b - b_ref)
    ki = k * jnp.exp(b_ref - b)
    A = jnp.einsum('bnchk,bnshk->bnhcs', qi, ki)
    A = jnp.where(jnp.tril(jnp.ones((C, C), dtype=bool)), A, 0.0)
    o_intra = jnp.einsum('bnhcs,bnshv->bnchv', A, v)
    b_last = b[:, :, -1:]
    kv = jnp.einsum('bnchk,bnchv->bnhkv', k * jnp.exp(b_last - b), v)
    decay = jnp.exp(b_last[:, :, 0])

    def step(S, inp):
        d, u = inp
        return S * d[..., None] + u, S

    _, S_prev = lax.scan(step, jnp.zeros((B_, H, K, V), f32),
                         (jnp.moveaxis(decay, 1, 0), jnp.moveaxis(kv, 1, 0)))
    S_prev = jnp.moveaxis(S_prev, 0, 1)
    o_inter = jnp.einsum('bnchk,bnhkv->bnchv', q * jnp.exp(b), S_prev)
    return (o_intra + o_inter).reshape(B_, T, H, V)


def _gla_branch(p, wa2_f, ba_f, wa2_b, ba_b, norm_g, proj):
    f32 = jnp.float32
    B_, T, _ = p.shape
    q, k, v, og, af, ab = _split(p, GLA_SPLITS)
    q = q.reshape(B_, T, GLA_HEADS, GLA_DK) * (GLA_DK ** -0.5)
    k = k.reshape(B_, T, GLA_HEADS, GLA_DK)
    v = v.reshape(B_, T, GLA_HEADS, GLA_DV)
    la_f = (jax.nn.log_sigmoid((af @ wa2_f + ba_f).astype(f32)) / GLA_LOGIT_NORM).reshape(B_, T, GLA_HEADS, GLA_DK)
    la_b = (jax.nn.log_sigmoid((ab @ wa2_b + ba_b).astype(f32)) / GLA_LOGIT_NORM).reshape(B_, T, GLA_HEADS, GLA_DK)
    flip = lambda t: jnp.flip(t, axis=1)
    o = _gla_chunked(q, k, v, la_f) + flip(_gla_chunked(flip(q), flip(k), flip(v), flip(la_b)))
    o = o * lax.rsqrt(jnp.mean(o * o, axis=-1, keepdims=True) + HEAD_NORM_EPS)
    o = (o.reshape(B_, T, GLA_V_W) * norm_g.astype(f32)).astype(p.dtype)
    o = o * jax.nn.silu(og)
    return o @ proj


def _rwkv7_step(S, inp):
    r, w, k, v, a, b = inp
    sa = jnp.einsum('bhvk,bhk->bhv', S, a)
    S = S * w[:, :, None, :] + sa[..., None] * b[:, :, None, :] + v[..., None] * k[:, :, None, :]
    y = jnp.einsum('bhvk,bhk->bhv', S, r)
    return S, y


def _rwkv_branch(p, mu_prev, mu_next, w0_f, w2_f, w0_b, w2_b, a0, a2, g2, k_k, k_a, r_k, ln_w, ln_b, proj):
    f32 = jnp.float32
    B_, T, _ = p.shape
    s = p + mu_prev * (_shift_prev(p) - p) + mu_next * (_shift_next(p) - p)
    r, k, v, wl, al, gl = _split(s, RWKV_SPLITS)
    tw = jnp.tanh(wl)

    def decay(w0, w2):
        w = -jax.nn.softplus(-(w0 + tw @ w2).astype(f32)) - 0.5
        return jnp.exp(-jnp.exp(w))

    a = jax.nn.sigmoid(a0 + al @ a2)
    g = jax.nn.sigmoid(gl) @ g2
    heads = lambda t: t.reshape(B_, T, RWKV_HEADS, RWKV_HEAD).astype(f32)
    kk = heads(k * k_k)
    kk = kk / jnp.maximum(jnp.sqrt(jnp.sum(kk * kk, axis=-1, keepdims=True)), 1e-12)
    k = k * (1.0 + (a - 1.0) * k_a)
    rh, kh, vh, ah = heads(r), heads(k), heads(v), heads(a)
    tm = lambda t: jnp.moveaxis(t, 1, 0)
    r_s, k_s, v_s, a_s, b_s = tm(rh), tm(kh), tm(vh), tm(-kk), tm(kk * ah)
    S0 = jnp.zeros((B_, RWKV_HEADS, RWKV_HEAD, RWKV_HEAD), f32)
    _, y_f = lax.scan(_rwkv7_step, S0, (r_s, tm(heads(decay(w0_f, w2_f))), k_s, v_s, a_s, b_s))
    _, y_b = lax.scan(_rwkv7_step, S0, (r_s, tm(heads(decay(w0_b, w2_b))), k_s, v_s, a_s, b_s), reverse=True)
    y = jnp.moveaxis(y_f + y_b, 0, 1)
    mu = jnp.mean(y, axis=-1, keepdims=True)
    var = jnp.mean(jnp.square(y - mu), axis=-1, keepdims=True)
    y = ((y - mu) * lax.rsqrt(var + RWKV_GN_EPS)).reshape(B_, T, RWKV_W) * ln_w.astype(f32) + ln_b.astype(f32)
    bonus = (jnp.sum(rh * kh * r_k.astype(f32), axis=-1, keepdims=True) * vh).reshape(B_, T, RWKV_W)
    o = (y + bonus).astype(p.dtype) * g
    return o @ proj


def setup_inputs(seed: int = 0) -> dict:
    key = jax.random.key(seed)
    ks = jax.random.split(key, 32)
    L, D = DEPTH, D_MODEL
    f32 = jnp.float32
    nrm = lambda k, shape, scale: jax.random.normal(k, shape, f32) * scale
    uni = lambda k, shape: jax.random.uniform(k, shape, f32, 0.0, 0.5)
    centre = jnp.array([0.0, 1.0, 0.0], f32)[None, :, None]
    return {
        "x": nrm(ks[0], (BATCH, SEQ, D), 1.0),
        "norm1_g": 1.0 + nrm(ks[1], (L, D), 0.02),
        "w_in": nrm(ks[2], (L, D, N_PROJ), D ** -0.5),
        "gla_wa2_f": nrm(ks[3], (L, GLA_GATE_RANK, GLA_QK_W), GLA_GATE_RANK ** -0.5),
        "gla_ba_f": 1.0 + nrm(ks[4], (L, GLA_QK_W), 0.5),
        "gla_wa2_b": nrm(ks[5], (L, GLA_GATE_RANK, GLA_QK_W), GLA_GATE_RANK ** -0.5),
        "gla_ba_b": 1.0 + nrm(ks[6], (L, GLA_QK_W), 0.5),
        "gla_norm_g": 1.0 + nrm(ks[7], (L, GLA_V_W), 0.02),
        "gla_proj": nrm(ks[8], (L, GLA_V_W, D), GLA_V_W ** -0.5),
        "rwkv_mu_prev": uni(ks[9], (L, RWKV_PROJ_W)),
        "rwkv_mu_next": uni(ks[10], (L, RWKV_PROJ_W)),
        "rwkv_w0_f": -1.0 + nrm(ks[11], (L, RWKV_W), 0.5),
        "rwkv_w2_f": nrm(ks[12], (L, RWKV_DECAY_LORA, RWKV_W), RWKV_DECAY_LORA ** -0.5),
        "rwkv_w0_b": -1.0 + nrm(ks[13], (L, RWKV_W), 0.5),
        "rwkv_w2_b": nrm(ks[14], (L, RWKV_DECAY_LORA, RWKV_W), RWKV_DECAY_LORA ** -0.5),
        "rwkv_a0": nrm(ks[15], (L, RWKV_W), 0.1),
        "rwkv_a2": nrm(ks[16], (L, RWKV_AAA_LORA, RWKV_W), RWKV_AAA_LORA ** -0.5),
        "rwkv_g2": nrm(ks[17], (L, RWKV_GATE_LORA, RWKV_W), RWKV_GATE_LORA ** -0.5),
        "rwkv_k_k": 0.85 + nrm(ks[18], (L, RWKV_W), 0.05),
        "rwkv_k_a": 1.0 + nrm(ks[19], (L, RWKV_W), 0.05),
        "rwkv_r_k": nrm(ks[20], (L, RWKV_HEADS, RWKV_HEAD), 0.1),
        "rwkv_ln_w": 1.0 + nrm(ks[21], (L, RWKV_W), 0.02),
        "rwkv_ln_b": nrm(ks[22], (L, RWKV_W), 0.02),
        "rwkv_proj": nrm(ks[23], (L, RWKV_W, D), RWKV_W ** -0.5),
        "w_out": nrm(ks[24], (L, D, D), D ** -0.5),
        "norm2_g": 1.0 + nrm(ks[25], (L, D), 0.02),
        "ffn_up": nrm(ks[26], (L, D, 2 * D_FF), D ** -0.5),
        "ffn_conv_w": centre + nrm(ks[27], (L, CONV_W, 2 * D_FF), 0.2),
        "ffn_conv_b": nrm(ks[28], (L, 2 * D_FF), 0.02),
        "ffn_down": nrm(ks[29], (L, D_FF, D), D_FF ** -0.5),
        "norm_f_g": 1.0 + nrm(ks[30], (D,), 0.02),
    }


def reference(x, norm1_g, w_in, gla_wa2_f, gla_ba_f, gla_wa2_b, gla_ba_b, gla_norm_g, gla_proj,
              rwkv_mu_prev, rwkv_mu_next, rwkv_w0_f, rwkv_w2_f, rwkv_w0_b, rwkv_w2_b, rwkv_a0, rwkv_a2,
              rwkv_g2, rwkv_k_k, rwkv_k_a, rwkv_r_k, rwkv_ln_w, rwkv_ln_b, rwkv_proj, w_out,
              norm2_g, ffn_up, ffn_conv_w, ffn_conv_b, ffn_down, norm_f_g):
    for l in range(DEPTH):
        h = _rmsnorm(x, norm1_g[l])
        p = h @ w_in[l]
        p_gla, p_rwkv, p_gate = jnp.split(p, [GLA_PROJ_W, GLA_PROJ_W + RWKV_PROJ_W], axis=-1)
        y_a = _gla_branch(p_gla, gla_wa2_f[l], gla_ba_f[l], gla_wa2_b[l], gla_ba_b[l],
                          gla_norm_g[l], gla_proj[l])
        y_b = _rwkv_branch(p_rwkv, rwkv_mu_prev[l], rwkv_mu_next[l], rwkv_w0_f[l], rwkv_w2_f[l],
                           rwkv_w0_b[l], rwkv_w2_b[l], rwkv_a0[l], rwkv_a2[l], rwkv_g2[l],
                           rwkv_k_k[l], rwkv_k_a[l], rwkv_r_k[l], rwkv_ln_w[l], rwkv_ln_b[l], rwkv_proj[l])
        gate_a, gate_b = jnp.split(p_gate, 2, axis=-1)
        merged = jax.nn.sigmoid(gate_a) * y_a + jax.nn.sigmoid(gate_b) * y_b
        x = x + merged @ w_out[l]
        h2 = _rmsnorm(x, norm2_g[l])
        u = h2 @ ffn_up[l]
        cw = ffn_conv_w[l]
        u = cw[0] * _shift_prev(u) + cw[1] * u + cw[2] * _shift_next(u) + ffn_conv_b[l]
        u_gate, u_val = jnp.split(u, 2, axis=-1)
        x = x + (jax.nn.silu(u_gate) * u_val) @ ffn_down[l]
    return _rmsnorm(x, norm_f_g)
```

```python
import numpy as np
import ml_dtypes
import concourse.bass as bass
import concourse.mybir as mybir
from concourse.bass_utils import run_bass_kernel_spmd

F32 = mybir.dt.float32
BF16 = mybir.dt.bfloat16
AF = mybir.ActivationFunctionType
ALU = mybir.AluOpType

D = 1024
NPROJ = 5408
DFF = 2752
SEM_EPOCH = 12000
N_CORES = 8
KT_ENG = "dve"
STOP_AFTER = None
RW_STOP = 99
PM_MODE = 2


class Res:
    __slots__ = ("name", "writers", "readers", "psum")

    def __init__(self, name, psum=False):
        self.name = name
        self.writers = []
        self.readers = []
        self.psum = psum


class Op:
    __slots__ = ("eng", "fn", "deps", "is_dma", "done", "need_inc", "slot", "idx")

    def __init__(self, eng, fn, is_dma, slot):
        self.eng = eng
        self.fn = fn
        self.deps = []
        self.is_dma = is_dma
        self.done = None
        self.need_inc = False
        self.slot = slot


class Prog:
    ENGS = ("pe", "act", "dve", "pool", "sp")

    def __init__(self, nc):
        self.nc = nc
        self.ops = {e: [] for e in self.ENGS}
        self.barrier_op = None
        self.last = {}
        self.dmas = []
        self.nrec = 0

    def op(self, eng, fn, reads=(), writes=(), dma_slot=None):
        is_dma = dma_slot is not None
        o = Op(eng, fn, is_dma, dma_slot)
        o.idx = self.nrec
        self.nrec += 1
        deps = []
        for r in reads:
            deps.extend(r.writers)
            if r.psum:
                deps.extend(x for x in r.readers if x.eng != eng)
        fills = []
        for r in writes:
            same_fill = (is_dma and r.writers and not r.readers
                         and all(w.is_dma and w.slot is dma_slot for w in r.writers))
            fills.append(same_fill)
            if not same_fill:
                deps.extend(r.writers)
            deps.extend(r.readers)
        seen = set()
        for d in deps:
            if d is o or id(d) in seen:
                continue
            seen.add(id(d))
            if d.eng == "pe" and eng == "pe" and not d.is_dma:
                continue
            o.deps.append(d)
            d.need_inc = True
        if self.barrier_op is not None and all(d is not self.barrier_op for d in o.deps):
            o.deps.append(self.barrier_op)
        self.last[eng] = o
        if is_dma:
            self.dmas.append(o)
        for r in reads:
            r.readers.append(o)
        for r, same_fill in zip(writes, fills):
            if same_fill:
                r.writers.append(o)
            else:
                r.writers = [o]
            r.readers = []
        self.ops[eng].append(o)
        return o

    def barrier(self, fn, eng="pool"):
        o = Op(eng, fn, False, None)
        o.idx = self.nrec
        self.nrec += 1
        seen = set()
        for d in list(self.last.values()) + self.dmas:
            if id(d) not in seen:
                seen.add(id(d))
                o.deps.append(d)
                d.need_inc = True
        o.need_inc = True
        self.ops[eng].append(o)
        self.barrier_op = o
        self.last = {eng: o}
        self.dmas = []

    def emit(self, final_wait_eng="sp"):
        nc = self.nc
        sems = {}
        keep = []

        def get_sem(key):
            if key not in sems:
                cm = nc.semaphore("s_%s_%s" % (key[0], key[1]))
                sems[key] = cm.__enter__()
                keep.append(cm)
            return sems[key]

        slot_ids, slot_cnt, keys = {}, {}, set()
        alld = sorted((o for e in self.ENGS for o in self.ops[e] if o.is_dma), key=lambda o: o.idx)
        for o in alld:
            sid = slot_ids.setdefault(id(o.slot), len(slot_ids))
            c = slot_cnt.get(sid, 0) + 16
            slot_cnt[sid] = c
            o.done = (("dma", sid), c)
            keys.add(o.done[0])
        for e in self.ENGS:
            cnt = 0
            for o in self.ops[e]:
                if o.is_dma:
                    pass
                elif o.need_inc:
                    cnt += 1
                    ep = (cnt - 1) // SEM_EPOCH
                    o.done = ((e, ep), cnt - ep * SEM_EPOCH)
                    keys.add(o.done[0])
        for k in sorted(keys, key=str):
            get_sem(k)
        final = dict(slot_cnt)
        engmap = {"pe": "tensor", "act": "scalar", "dve": "vector", "pool": "gpsimd", "sp": "sync"}
        with nc.Block() as block:
            for e in self.ENGS:
                ops = self.ops[e]
                is_final = (e == final_wait_eng)
                if not ops and not is_final:
                    continue

                def body(eng, ops=ops, is_final=is_final):
                    waited = {}
                    for o in ops:
                        need = {}
                        for d in o.deps:
                            k, v = d.done
                            if need.get(k, 0) < v:
                                need[k] = v
                        for k, v in need.items():
                            if waited.get(k, 0) >= v:
                                continue
                            waited[k] = v
                            eng.wait_ge(sems[k], v)
                        ins = o.fn(eng)
                        if o.is_dma:
                            ins.then_inc(sems[o.done[0]], 16)
                        elif o.need_inc:
                            ins.then_inc(sems[o.done[0]], 1)
                    if is_final:
                        for sid, c in final.items():
                            eng.wait_ge(sems[("dma", sid)], c)

                getattr(block, engmap[e])(body)


class V:
    __slots__ = ("ap", "res")

    def __init__(self, ap, res):
        self.ap = ap
        self.res = res

    def __getitem__(self, k):
        return V(self.ap[k], self.res)

    def re(self, pat, **kw):
        return V(self.ap.rearrange(pat, **kw), self.res)

    def bc(self, shape):
        return V(self.ap.to_broadcast(list(shape)), self.res)

    def bitcast(self, dt):
        return V(self.ap.bitcast(dt), self.res)


def _r(*vs):
    return [v.res for v in vs if isinstance(v, V) and v.res is not None]


def _a(x):
    return x.ap if isinstance(x, V) else x


class KB:
    def __init__(self, nc):
        self.nc = nc
        self.P = Prog(nc)
        self.rots = {}
        self.ARENA = 212000
        self.arena = nc.alloc_sbuf_tensor("arena", [128, self.ARENA // 4], F32)
        self.gptr = 0
        self.phase = None
        self.pptr = {}
        self.nps = 0
        self.banks = []
        for i in range(8):
            t = nc.alloc_psum_tensor("psb%d" % i, [128, 512], F32)
            self.banks.append(V(t[:], Res("psb%d" % i, psum=True)))
        self.uid = 0
        self.dummy = self.sb("dummy", [128, 8], F32)

    def sb(self, name, shape, dt):
        esz = 4 if dt == F32 else 2
        n = 1
        for d_ in shape[1:]:
            n *= d_
        nbytes = (n * esz + 31) // 32 * 32
        if self.phase is None:
            assert not self.pptr, "global allocation after first phase: " + name
            off = self.gptr
            self.gptr += nbytes
        else:
            off = self.pptr[self.phase]
            self.pptr[self.phase] += nbytes
        assert off + nbytes <= self.ARENA, ("SBUF arena overflow", name, self.phase, off, nbytes)
        self.last_off = off
        return self.view_at(name, off, shape, dt)

    def view_at(self, name, off, shape, dt):
        esz = 4 if dt == F32 else 2
        n = 1
        for d_ in shape[1:]:
            n *= d_
        nbytes = (n * esz + 31) // 32 * 32
        ap = self.arena[0:shape[0], off // 4:(off + nbytes) // 4]
        if dt != F32:
            ap = ap.bitcast(dt)
        ap = ap[:, 0:n]
        if len(shape) > 2:
            names = "abcdef"[:len(shape) - 1]
            pat = "p (%s) -> p %s" % (" ".join(names), " ".join(names))
            ap = ap.rearrange(pat, **{names[i]: shape[i + 1] for i in range(len(names))})
        return V(ap, Res(name))

    def set_phase(self, name):
        if name not in self.pptr:
            self.pptr[name] = self.gptr
        self.phase = name
        dummy = self.dummy
        self.P.barrier(lambda e: e.memset(dummy.ap, 0.0))

    def rot(self, tag, shape, dt, n=2):
        key = tag if (tag in self.rots or self.phase is None) else (self.phase, tag)
        if key not in self.rots:
            self.rots[key] = [[self.sb("%s_%d" % (tag, i), shape, dt) for i in range(n)], 0]
        lst = self.rots[key]
        v = lst[0][lst[1] % len(lst[0])]
        lst[1] += 1
        return v

    def psum(self):
        v = self.banks[self.nps % 7]
        self.nps += 1
        return v

    def dram(self, name, shape, dt, kind="Internal"):
        t = self.nc.dram_tensor(name, list(shape), dt, kind=kind)
        return t.ap()

    def mm(self, out, lhsT, rhs, start=True, stop=True):
        self.P.op("pe", lambda e: e.matmul(out.ap, lhsT=lhsT.ap, rhs=rhs.ap, start=start, stop=stop),
                  reads=_r(lhsT, rhs), writes=_r(out))

    def tr(self, out, in_, ident):
        self.P.op("pe", lambda e: e.transpose(out.ap, in_.ap, ident.ap), reads=_r(in_, ident), writes=_r(out))

    def act(self, out, in_, func, bias=0.0, scale=1.0, eng="act"):
        self.P.op(eng, lambda e: e.activation(out=out.ap, in_=in_.ap, func=func, bias=_a(bias), scale=_a(scale)),
                  reads=_r(in_, bias, scale), writes=_r(out))

    def tt(self, out, a, b, op, eng="dve"):
        self.P.op(eng, lambda e: e.tensor_tensor(out=out.ap, in0=a.ap, in1=b.ap, op=op), reads=_r(a, b), writes=_r(out))

    def ts(self, out, a, s1, s2, op0, op1=None, eng="dve"):
        if op1 is None:
            self.P.op(eng, lambda e: e.tensor_scalar(out=out.ap, in0=a.ap, scalar1=_a(s1), scalar2=None, op0=op0),
                      reads=_r(a, s1), writes=_r(out))
        else:
            self.P.op(eng, lambda e: e.tensor_scalar(out=out.ap, in0=a.ap, scalar1=_a(s1), scalar2=_a(s2), op0=op0, op1=op1),
                      reads=_r(a, s1, s2), writes=_r(out))

    def stt(self, out, in0, scalar, in1, op0, op1, eng="dve"):
        self.P.op(eng, lambda e: e.scalar_tensor_tensor(out=out.ap, in0=in0.ap, scalar=_a(scalar), in1=in1.ap, op0=op0, op1=op1),
                  reads=_r(in0, scalar, in1), writes=_r(out))

    def copy(self, out, in_, eng="act"):
        if eng == "act":
            self.P.op("act", lambda e: e.copy(out=out.ap, in_=in_.ap), reads=_r(in_), writes=_r(out))
        else:
            self.P.op(eng, lambda e: e.tensor_copy(out=out.ap, in_=in_.ap), reads=_r(in_), writes=_r(out))

    def scan(self, out, d0, d1):
        self.P.op("dve", lambda e: e.tensor_tensor_scan(out=out.ap, data0=d0.ap, data1=d1.ap, initial=0.0,
                                                        op0=ALU.mult, op1=ALU.add), reads=_r(d0, d1), writes=_r(out))

    def memset(self, out, val, eng="pool"):
        self.P.op(eng, lambda e: e.memset(out.ap, val), writes=_r(out))

    def dma(self, out, in_, q="sp", slow=False):
        o_sb = out.ap.space is not None and "DRAM" not in str(out.ap.space).upper() and "HBM" not in str(out.ap.space).upper()
        slot = out.res if o_sb else in_.res
        kw = {"allow_slow_non_contiguous": True} if slow else {}
        self.P.op(q, lambda e: e.dma_start(out=out.ap, in_=in_.ap, **kw), reads=_r(in_), writes=_r(out), dma_slot=slot)


def _fm(v, nch):
    return np.ascontiguousarray(np.asarray(v, np.float32).reshape(nch, 128).T)


PV_SPEC = [("g1", 8), ("g2n", 8), ("gf", 8), ("ba_f", 2), ("ba_b", 2), ("gng", 4), ("mu_p", 14), ("mu_n", 14),
           ("w0_f", 4), ("w0_b", 4), ("a0", 4), ("k_k", 4), ("k_a", 4), ("r_k", 4), ("ln_w", 4), ("ln_b", 4),
           ("cwg", 66), ("cwv", 66), ("cbg", 22), ("cbv", 22)]
PV_OFF = {}
_o = 0
for _n, _w in PV_SPEC:
    PV_OFF[_n] = (_o, _w)
    _o += _w
PV_W = _o


def make_consts():
    p = np.arange(128) % 64
    j = np.arange(64)
    su = (j[None, :] > p[:, None]).astype(np.float32)
    iu = (j[None, :] >= p[:, None]).astype(np.float32)
    sl = (j[None, :] < p[:, None]).astype(np.float32)
    il = (j[None, :] <= p[:, None]).astype(np.float32)
    eye = (j[None, :] == p[:, None]).astype(np.float32)
    c = {}
    c["ident"] = np.eye(128, dtype=np.float32).astype(ml_dtypes.bfloat16)
    c["ones"] = np.ones((128, 128), np.float32).astype(ml_dtypes.bfloat16)
    bo = np.zeros((128, 128), np.float32)
    bo[:64, :64] = 1
    bo[64:, 64:] = 1
    c["bones"] = bo.astype(ml_dtypes.bfloat16)
    cm = np.ones((128, 1024), np.float32)
    cm[:, ::64] = 0
    c["cmask"] = cm
    c["masks"] = np.ascontiguousarray(np.stack([su, iu, sl, il, eye], 1))
    return c


def build(nc, NSEQ, T):
    kb = KB(nc)
    TB = min(256, T)
    NB = T // TB
    TBF = min(512, T)
    NBF = T // TBF
    CH = 64
    NCH = TB // CH
    R0 = 1568
    EI = "ExternalInput"
    xT = kb.dram("xT", [NSEQ, D, T], F32, kind=EI)
    pvec_d = kb.dram("pvec", [128, PV_W], F32, kind=EI)
    ident_d = kb.dram("ident", [128, 128], BF16, kind=EI)
    ones_d = kb.dram("ones", [128, 128], BF16, kind=EI)
    bones_d = kb.dram("bones", [128, 128], BF16, kind=EI)
    cmask_d = kb.dram("cmask", [128, 1024], F32, kind=EI)
    masks_d = kb.dram("masks", [128, 5, 64], F32, kind=EI)
    w_in = kb.dram("w_in", [D, NPROJ], F32, kind=EI)
    wa2_d = [kb.dram("wa2_f", [16, 256], F32, kind=EI), kb.dram("wa2_b", [16, 256], F32, kind=EI)]
    w2_d = [kb.dram("w2_f", [64, 512], F32, kind=EI), kb.dram("w2_b", [64, 512], F32, kind=EI)]
    a2_d = kb.dram("a2", [64, 512], F32, kind=EI)
    g2_d = kb.dram("g2", [128, 512], F32, kind=EI)
    glap_d = kb.dram("gla_proj", [512, D], F32, kind=EI)
    rwp_d = kb.dram("rwkv_proj", [512, D], F32, kind=EI)
    wout_d = kb.dram("w_out", [D, D], F32, kind=EI)
    fup_d = kb.dram("ffn_up", [D, 2 * DFF], F32, kind=EI)
    fdn_d = kb.dram("ffn_down", [DFF, D], F32, kind=EI)
    outT = kb.dram("outT", [NSEQ, D, T], F32, kind="ExternalOutput")
    oA_s = kb.dram("oA_s", [NSEQ, 2, 128, 4, T], F32)
    yB_s = kb.dram("yB_s", [NSEQ, 2, 128, 4, T], F32)
    bon_s = kb.dram("bon_s", [NSEQ, 128, 4, T], F32)
    sgl_s = kb.dram("sgl_s", [NSEQ, 128, T], BF16)
    x1_s = kb.dram("x1_s", [NSEQ, D, T], F32)
    rf_s = kb.dram("rf_s", [NSEQ, 128, 4, T], F32)
    kf_s = kb.dram("kf_s", [NSEQ, 128, 4, T], F32)
    vt_s = kb.dram("vt_s", [NSEQ, 128, 4, T], BF16)
    wa_s = kb.dram("wa_s", [NSEQ, 128, T], F32)
    gq_s = kb.dram("gq_s", [NSEQ, 128, 2, T], F32)
    kk_s = kb.dram("kk_s", [NSEQ, 128, 4, T], F32)
    gv_s = kb.dram("gv_s", [NSEQ, 128, T // 64, 256], BF16)
    km_s = kb.dram("km_s", [NSEQ, 128, 4, T], F32)
    be_s = kb.dram("be_s", [NSEQ, 128, 4, T], F32)
    gk_s = kb.dram("gk_s", [NSEQ, 128, 2, T], F32)
    w_in_v = w_in.rearrange("(kc p) c -> p kc c", p=128)
    glap_v = glap_d.rearrange("(kc p) c -> p kc c", p=128)
    rwp_v = rwp_d.rearrange("(kc p) c -> p kc c", p=128)
    wout_v = wout_d.rearrange("(kc p) c -> p kc c", p=128)
    fup_v = fup_d.rearrange("(kc p) c -> p kc c", p=128)

    def DV(ap):
        return V(ap, None)

    scr = {}

    def SV(name, s, d_, ap):
        k = (name, s, d_)
        if k not in scr:
            scr[k] = Res("scr_%s_%d_%d" % k)
        return V(ap, scr[k])

    pvec = kb.sb("pvec", [128, PV_W], F32)
    ident = kb.sb("ident", [128, 128], BF16)
    ones = kb.sb("ones", [128, 128], BF16)
    bones = kb.sb("bones", [128, 128], BF16)
    cmask = kb.sb("cmask", [128, 1024], F32)
    masks = kb.sb("masks", [128, 5, 64], F32)
    for dst, src in ((pvec, pvec_d), (ident, ident_d), (ones, ones_d), (bones, bones_d), (cmask, cmask_d), (masks, masks_d)):
        kb.dma(dst, DV(src), q="sp")

    def pv(name, j=None):
        o, w = PV_OFF[name]
        if j is None:
            return pvec[:, o:o + w]
        return pvec[:, o + j:o + j + 1]

    hT = kb.sb("hT", [128, 8, T + 2], BF16)
    hT_off = kb.last_off
    hT_bytes = 8 * (T + 2) * 2
    kb.rot("wbf", [128, 8, 512], BF16, n=3)
    gf32 = kb.sb("gf32", [128, 8], F32)
    kb.ts(gf32, pv("gf"), 32.0, None, ALU.mult)

    def rsqrt(out, in_, add):
        kb.act(out, in_, AF.Ln, bias=add)
        kb.act(out, out, AF.Exp, scale=-0.5)

    Win_s = kb.dram("Win_s", [128, 8, NPROJ], BF16)
    glap_s = kb.dram("glap_s", [128, 4, D], BF16)
    rwp_s = kb.dram("rwp_s", [128, 4, D], BF16)
    wout_s = kb.dram("wout_s", [128, 8, D], BF16)
    fup_s = kb.dram("fup_s", [128, 8, 2 * DFF], BF16)

    def precast(view, kcn, ncols_total, dst_s, gname):
        for c0 in range(0, ncols_total, 512):
            n = min(512, ncols_total - c0)
            st = kb.rot("pst", [128, 8, 512], F32, n=2)
            kb.dma(st[:, 0:kcn, 0:n], DV(view[:, 0:kcn, c0:c0 + n]), q="sp")
            wb = kb.rot("pbf", [128, 8, 512], BF16, n=2)
            if gname is None:
                h_ = max(kcn // 2, 1)
                kb.copy(wb[:, 0:h_, 0:n], st[:, 0:h_, 0:n], eng="act")
                kb.copy(wb[:, h_:kcn, 0:n], st[:, h_:kcn, 0:n], eng="pool")
            else:
                for kc in range(kcn):
                    if kc % 3 == 0:
                        kb.act(wb[:, kc, 0:n], st[:, kc, 0:n], AF.Copy, scale=pv(gname, kc))
                    elif kc % 3 == 1:
                        kb.ts(wb[:, kc, 0:n], st[:, kc, 0:n], pv(gname, kc), None, ALU.mult)
                    else:
                        kb.ts(wb[:, kc, 0:n], st[:, kc, 0:n], pv(gname, kc), None, ALU.mult, eng="pool")
            kb.dma(DV(dst_s[:, 0:kcn, c0:c0 + n]), wb[:, 0:kcn, 0:n], q="act")

    def loadwx(view_s, kcn, c0, ncols):
        wb = kb.rot("wbf", [128, 8, 512], BF16, n=3)
        kb.dma(wb[:, 0:kcn, 0:ncols], DV(view_s[:, 0:kcn, c0:c0 + ncols]), q="sp")
        return wb

    def loadw(c0, ncols):
        return loadwx(Win_s, 8, c0, ncols)

    g2st = kb.sb("g2st", [128, 512], F32)
    kb.dma(g2st, DV(g2_d), q="sp")
    g2b = kb.sb("g2b", [128, 512], BF16)
    kb.copy(g2b, g2st, eng="pool")
    wd_s = kb.dram("wd_s", [128, 22, D], BF16)
    fdn_v = fdn_d[0:21 * 128, :].rearrange("(j p) c -> p j c", p=128)

    def load_Wd():
        Wd = kb.rot("p_Wd", [128, 22, D], BF16, n=1)
        kb.memset(Wd[:, 21, :], 0.0)
        for j0, jn in ((0, 8), (8, 8), (16, 5)):
            for hf in range(2):
                st = kb.rot("pst", [128, 8, 512], F32, n=2)
                kb.dma(st[:, 0:jn, :], DV(fdn_v[:, j0:j0 + jn, hf * 512:hf * 512 + 512]), q="sp")
                kb.copy(Wd[:, j0:j0 + jn, hf * 512:hf * 512 + 512], st[:, 0:jn, :], eng=("act" if hf == 0 else "pool"))
        for hf in range(2):
            st = kb.rot("pst", [128, 8, 512], F32, n=2)
            kb.dma(st[0:64, 0, :], DV(fdn_d[21 * 128:DFF, hf * 512:hf * 512 + 512]), q="sp")
            kb.copy(Wd[0:64, 21, hf * 512:hf * 512 + 512], st[0:64, 0, :], eng="act")
        kb.dma(DV(wd_s[:, :, :]), Wd, q="act")

    wa2 = []
    for d_ in range(2):
        st = kb.sb("wa2st%d" % d_, [16, 256], F32)
        kb.dma(st, DV(wa2_d[d_]), q="sp")
        wb = kb.sb("wa2b%d" % d_, [16, 256], BF16)
        kb.copy(wb, st, eng="pool")
        wa2.append(wb)
    nba = kb.sb("nba", [128, 4], F32)
    kb.ts(nba[:, 0:2], pv("ba_f"), -1.0, None, ALU.mult)
    kb.ts(nba[:, 2:4], pv("ba_b"), -1.0, None, ALU.mult)
    S32 = kb.sb("S32", [128, 2, 128], F32)
    Sb = kb.sb("Sb", [128, 2, 128], BF16)

    def hs(h):
        return slice((h % 2) * 64, (h % 2) * 64 + 64)

    def gla_sweep(s, dirn, preW=None):
        kb.memset(S32, 0.0)
        kb.memset(Sb, 0.0)
        mA = masks[:, 1:2, :] if dirn == 0 else masks[:, 3:4, :]
        blocks = range(NB) if dirn == 0 else range(NB - 1, -1, -1)
        for b in blocks:
            t0 = b * TB
            hblk = lambda kc: hT[:, kc, 1 + t0:1 + t0 + TB]
            q_f = kb.rot("q_f", [128, 2, TB], F32, n=2)
            k_f = kb.rot("k_f", [128, 2, TB], F32, n=2)
            if dirn == 0:
                Wqk = preW if (b == 0 and preW is not None) else loadw(0, 512)
                for i, dst in ((0, q_f), (1, k_f)):
                    for pc in range(2):
                        ps = kb.psum()
                        for kc in range(8):
                            kb.mm(ps[:, 0:TB], Wqk[:, kc, i * 256 + pc * 128:i * 256 + pc * 128 + 128], hblk(kc), start=(kc == 0), stop=(kc == 7))
                        kb.copy(dst[:, pc, :], ps[:, 0:TB], eng="act")
                kb.dma(SV("gq", s, 0, gq_s[s, :, :, t0:t0 + TB]), q_f, q="act")
                kb.dma(SV("gk", s, 0, gk_s[s, :, :, t0:t0 + TB]), k_f, q="act")
            else:
                kb.dma(q_f, SV("gq", s, 0, gq_s[s, :, :, t0:t0 + TB]), q="sp")
                kb.dma(k_f, SV("gk", s, 0, gk_s[s, :, :, t0:t0 + TB]), q="sp")
            Wa = loadw(1536 + 16 * dirn, 16)
            ps = kb.psum()
            for kc in range(8):
                kb.mm(ps[0:16, 0:TB], Wa[:, kc, 0:16], hblk(kc), start=(kc == 0), stop=(kc == 7))
            afT = kb.rot("afT", [16, TB], BF16, n=1)
            kb.copy(afT, ps[0:16, 0:TB], eng="act")
            cs = kb.rot("gcs", [128, 2, TB], F32, n=1)
            lg = kb.rot("glg", [128, 2, TB], F32, n=1)
            G = kb.rot("gG", [128, 2, TB], F32, n=1)
            Gi = kb.rot("gGi", [128, 2, TB], F32, n=1)
            qbT = kb.rot("qbT", [128, 2, TB], BF16, n=1)
            ktT = kb.rot("ktT", [128, 2, TB], BF16, n=1)
            gC = kb.rot("ggC", [128, 2, NCH], F32, n=1)
            for pc in range(2):
                ps = kb.psum()
                kb.mm(ps[:, 0:TB], wa2[dirn][0:16, pc * 128:pc * 128 + 128], afT[0:16, :])
                kb.act(lg[:, pc, :], ps[:, 0:TB], AF.Exp, bias=nba[:, 2 * dirn + pc:2 * dirn + pc + 1], scale=-1.0)
                kb.act(lg[:, pc, :], lg[:, pc, :], AF.Ln, bias=1.0)
                kb.scan(cs[:, pc, :], cmask[:, 0:TB], lg[:, pc, :])
                if dirn == 1:
                    c3 = cs[:, pc, :].re("p (n c) -> p n c", c=CH)
                    l3 = lg[:, pc, :].re("p (n c) -> p n c", c=CH)
                    tot = kb.rot("gtot", [128, NCH, 1], F32, n=1)
                    kb.copy(tot, c3[:, :, CH - 1:CH], eng="pool")
                    kb.tt(c3, l3, c3, ALU.subtract)
                    kb.tt(c3, c3, tot.bc([128, NCH, CH]), ALU.add)
                kb.act(G[:, pc, :], cs[:, pc, :], AF.Exp, scale=-1.0 / 16.0)
                kb.act(Gi[:, pc, :], cs[:, pc, :], AF.Exp, scale=1.0 / 16.0)
                kb.stt(qbT[:, pc, :], q_f[:, pc, :], 0.125, G[:, pc, :], ALU.mult, ALU.mult)
                kb.tt(ktT[:, pc, :], k_f[:, pc, :], Gi[:, pc, :], ALU.mult, eng="pool")
                g3 = G[:, pc, :].re("p (n c) -> p n c", c=CH)
                edge = CH - 1 if dirn == 0 else 0
                kb.copy(gC[:, pc, :].re("p (n o) -> p n o", o=1), g3[:, :, edge:edge + 1], eng="pool")
            vblk = kb.rot("gvblk", [128, NCH, 2, 128], BF16, n=2)
            gch = slice(b * NCH, (b + 1) * NCH)
            if dirn == 0:
                Wv = loadw(512, 512)
                Wv4 = Wv.re("p k (pr hf v) -> p k pr hf v", pr=2, hf=2)
            else:
                kb.dma(vblk.re("p n a v -> p n (a v)"), SV("gv", s, 0, gv_s[s, :, gch, :]), q="sp")
            oblk = kb.rot("oblk", [128, 4, TB], F32, n=1)
            chunks = range(NCH) if dirn == 0 else range(NCH - 1, -1, -1)
            for ch in chunks:
                cc = slice(ch * CH, (ch + 1) * CH)
                tc0 = 1 + t0 + ch * CH
                vTM = vblk[:, ch]
                if dirn == 0:
                    psV = kb.psum()
                    for half in range(2):
                        for kc in range(8):
                            kb.mm(psV[half * 64:half * 64 + 64, 0:256].re("p (pr v) -> p pr v", pr=2),
                                  hT[:, kc, tc0:tc0 + CH], Wv4[:, kc, :, half, :], start=(kc == 0), stop=(kc == 7))
                    kb.copy(vTM, psV[:, 0:256].re("p (pr v) -> p pr v", pr=2), eng="act")
                psT = kb.psum().bitcast(BF16)
                for h in range(4):
                    kb.tr(psT[hs(h), (h // 2) * 64:(h // 2) * 64 + 64], ktT[hs(h), h // 2, cc], ident[hs(h), hs(h)])
                kTM = kb.rot("gkTM", [128, 2, 64], BF16, n=2)
                kb.copy(kTM, psT[:, 0:128].re("p (pr k) -> p pr k", pr=2), eng="act")
                psA = kb.psum()
                for h in range(4):
                    kb.mm(psA[hs(h), (h // 2) * 64:(h // 2) * 64 + 64], ktT[hs(h), h // 2, cc], qbT[hs(h), h // 2, cc])
                ATg = kb.rot("gAT", [128, 2, 64], BF16, n=2)
                kb.tt(ATg, psA[:, 0:128].re("p (pr c) -> p pr c", pr=2), mA.bc([128, 2, 64]), ALU.mult)
                psO2 = [kb.psum(), kb.psum()]
                for h in range(4):
                    po = psO2[h % 2][:, (h // 2) * 64:(h // 2) * 64 + 64]
                    kb.mm(po, vTM[hs(h), h // 2, :], ATg[hs(h), h // 2, :], start=True, stop=False)
                    kb.mm(po, Sb[hs(h), h // 2, :], qbT[hs(h), h // 2, cc], start=False, stop=True)
                ob4 = oblk.re("p (pr hf) t -> p pr hf t", hf=2)
                for hf_ in range(2):
                    kb.copy(ob4[:, :, hf_, cc], psO2[hf_][:, 0:128].re("p (pr c) -> p pr c", pr=2), eng="act")
                psK = kb.psum()
                for h in range(4):
                    kb.mm(psK[hs(h), (h // 2) * 128:(h // 2) * 128 + 128], kTM[hs(h), h // 2, :], vTM[hs(h), h // 2, :])
                kb.tt(S32, psK[:, 0:256].re("p (pr v) -> p pr v", pr=2), S32, ALU.add)
                kb.tt(S32, S32, gC[:, :, ch:ch + 1].bc([128, 2, 128]), ALU.mult, eng="pool")
                kb.copy(Sb, S32, eng="pool")
            kb.dma(SV("oA", s, dirn, oA_s[s, dirn, :, :, t0:t0 + TB]), oblk, q="act")
            if dirn == 0:
                kb.dma(SV("gv", s, 0, gv_s[s, :, gch, :]), vblk.re("p n a v -> p n (a v)"), q="act")


    R0 = 1568
    wsm = kb.sb("wsm_st", [128, 3, 512], F32)
    kb.dma(wsm[0:64, 0, :], DV(w2_d[0]), q="sp")
    kb.dma(wsm[0:64, 1, :], DV(w2_d[1]), q="sp")
    kb.dma(wsm[64:128, 2, :], DV(a2_d), q="sp")
    w2b = kb.sb("w2b", [128, 2, 512], BF16)
    a2b = kb.sb("a2b", [128, 512], BF16)
    kb.copy(w2b[0:64, :, :], wsm[0:64, 0:2, :], eng="pool")
    kb.copy(a2b[64:128, :], wsm[64:128, 2, :], eng="pool")
    muc = kb.sb("muc", [128, 14], F32)
    kb.tt(muc, pv("mu_p"), pv("mu_n"), ALU.add)
    kb.ts(muc, muc, -1.0, 1.0, ALU.mult, ALU.add)
    omk = kb.sb("omk", [128, 4], F32)
    kb.ts(omk, pv("k_a"), -1.0, 1.0, ALU.mult, ALU.add)
    H32 = kb.sb("H32", [128, 4, 64], F32)
    Hb = kb.sb("Hb", [128, 4, 64], BF16)
    LW = 0.6065306597126334

    def rwkv_sweep(s, dirn, preW=None):
        kb.memset(H32, 0.0)
        kb.memset(Hb, 0.0)
        if dirn == 0:
            m1 = masks[:, 0:2, :]
            mN = masks[:, 2:3, :]
        else:
            m1 = masks[:, 2:4, :]
            mN = masks[:, 0:1, :]
        eye = masks[:, 4:5, :]
        w0n = "w0_f" if dirn == 0 else "w0_b"
        blocks = range(NB) if dirn == 0 else range(NB - 1, -1, -1)
        for b in blocks:
            t0 = b * TB
            psh = kb.banks[7]
            vT = kb.rot("r_vT", [128, 4, TB], BF16, n=2)
            r_f = kb.rot("r_rf", [128, 4, TB], F32, n=2)
            k_f = kb.rot("r_kf", [128, 4, TB], F32, n=2)
            wa_f = kb.rot("r_wa", [128, TB], F32, n=2)

            def proj_mix(Wt, wc, cc, dst):
                ps = kb.psum()
                for kc in range(8):
                    kb.mm(ps[:, 0:TB], Wt[:, kc, wc:wc + 128], hT[:, kc, 1 + t0:1 + t0 + TB], start=(kc == 0), stop=(kc == 7))
                if PM_MODE >= 1:
                    for kc in range(8):
                        kb.mm(psh[:, 2 * cc:2 * cc + 2], Wt[:, kc, wc:wc + 128], hT[:, kc, t0:t0 + TB + 2:TB + 1], start=(kc == 0), stop=(kc == 7))
                Psb = kb.rot("r_Psb", [128, TB + 2], F32, n=2)
                kb.copy(Psb[:, 1:TB + 1], ps[:, 0:TB], eng="act")
                if PM_MODE >= 1:
                    kb.copy(Psb[:, 0:TB + 2:TB + 1], psh[:, 2 * cc:2 * cc + 2], eng="act")
                if PM_MODE >= 2:
                    s1 = kb.rot("r_s1", [128, TB], F32, n=2)
                    kb.ts(s1, ps[:, 0:TB], muc[:, cc:cc + 1], None, ALU.mult)
                    kb.stt(s1, Psb[:, 0:TB], pv("mu_p", cc), s1, ALU.mult, ALU.add)
                    kb.stt(dst, Psb[:, 2:TB + 2], pv("mu_n", cc), s1, ALU.mult, ALU.add)

            tsl_ = slice(t0, t0 + TB)
            if dirn == 0:
                for g_, dstt in ((0, r_f), (1, k_f), (2, vT)):
                    Wt = preW if (g_ == 0 and b == 0 and preW is not None) else loadw(R0 + g_ * 512, 512)
                    for pc in range(4):
                        proj_mix(Wt, pc * 128, g_ * 4 + pc, dstt[:, pc, :])
                Wt = loadw(R0 + 1536, 256)
                proj_mix(Wt, 0, 12, wa_f)
                kb.dma(SV("rf", s, 0, rf_s[s, :, :, tsl_]), r_f, q="act")
                kb.dma(SV("kf", s, 0, kf_s[s, :, :, tsl_]), k_f, q="act")
                kb.dma(SV("vt", s, 0, vt_s[s, :, :, tsl_]), vT, q="act")
                kb.dma(SV("wa", s, 0, wa_s[s, :, tsl_]), wa_f, q="act")
            else:
                kb.dma(r_f, SV("rf", s, 0, rf_s[s, :, :, tsl_]), q="sp")
                kb.dma(vT, SV("vt", s, 0, vt_s[s, :, :, tsl_]), q="sp")
                kb.dma(wa_f, SV("wa", s, 0, wa_s[s, :, tsl_]), q="sp")
            if dirn == 0:
                gl_f = kb.rot("r_gl", [128, TB], F32, n=1)
                proj_mix(Wt, 128, 13, gl_f)
                sglb = kb.rot("r_sglb", [128, TB], BF16, n=1)
                kb.act(sglb, gl_f, AF.Sigmoid)
                kb.dma(SV("sgl", s, 0, sgl_s[s, :, t0:t0 + TB]), sglb, q="act")
                bonblk = kb.rot("r_bon", [128, 4, TB], F32, n=1)
            if RW_STOP <= 1:
                continue
            twal = kb.rot("r_twal", [128, TB], BF16, n=1)
            kb.act(twal[0:64, :], wa_f[0:64, :], AF.Tanh)
            kb.copy(twal[64:128, :], wa_f[64:128, :], eng="pool")
            AR = kb.rot("r_AR", [128, 4, NCH, 128], BF16, n=1)
            ktT = kb.rot("r_ktT", [128, 4, TB], BF16, n=1)
            btT = kb.rot("r_btT", [128, 4, TB], BF16, n=1)
            gC = kb.rot("r_gC", [128, 4, NCH], F32, n=1)
            T4 = lambda tag, dt=F32: kb.rot("r_q_" + tag, [128, 4, TB], dt, n=1)
            bcp = lambda name: pv(name).re("p (a o) -> p a o", o=1).bc([128, 4, TB])
            sgA, aA = T4("sg"), T4("a")
            for pc in range(4):
                ps = kb.psum()
                kb.mm(ps[:, 0:TB], w2b[0:64, dirn, pc * 128:pc * 128 + 128], twal[0:64, :])
                kb.act(sgA[:, pc, :], ps[:, 0:TB], AF.Sigmoid, bias=pv(w0n, pc))
            if dirn == 0:
                for pc in range(4):
                    ps = kb.psum()
                    kb.mm(ps[:, 0:TB], a2b[64:128, pc * 128:pc * 128 + 128], twal[64:128, :])
                    kb.act(aA[:, pc, :], ps[:, 0:TB], AF.Sigmoid, bias=pv("a0", pc))
                kkA = T4("kk")
                kb.tt(kkA, k_f, bcp("k_k"), ALU.mult)
                sqb = T4("sqb", BF16)
                kb.tt(sqb, kkA, kkA, ALU.mult, eng="pool")
                rnA = T4("rn")
                for pc in range(4):
                    ps = kb.psum()
                    kb.mm(ps[:, 0:TB], bones, sqb[:, pc, :])
                    kb.act(rnA[:, pc, :], ps[:, 0:TB], AF.Ln, bias=1e-24)
                kb.act(rnA, rnA, AF.Exp, scale=-0.5)
                kb.tt(kkA, kkA, rnA, ALU.mult, eng="pool")
                taA = rnA
                kb.tt(taA, aA, bcp("k_a"), ALU.mult, eng="pool")
                kb.tt(taA, taA, omk.re("p (a o) -> p a o", o=1).bc([128, 4, TB]), ALU.add, eng="pool")
                kmodA = T4("kmod")
                kb.tt(kmodA, k_f, taA, ALU.mult)
                betaA = T4("beta")
                kb.tt(betaA, kkA, aA, ALU.mult, eng="pool")
                kb.dma(SV("kk", s, 0, kk_s[s, :, :, tsl_]), kkA, q="act")
                kb.dma(SV("km", s, 0, km_s[s, :, :, tsl_]), kmodA, q="act")
                kb.dma(SV("be", s, 0, be_s[s, :, :, tsl_]), betaA, q="act")
            else:
                kkA, kmodA, betaA = T4("kk"), T4("kmod"), T4("beta")
                kb.dma(kkA, SV("kk", s, 0, kk_s[s, :, :, tsl_]), q="sp")
                kb.dma(kmodA, SV("km", s, 0, km_s[s, :, :, tsl_]), q="sp")
                kb.dma(betaA, SV("be", s, 0, be_s[s, :, :, tsl_]), q="sp")
            if dirn == 0:
                rkb = T4("rkb", BF16)
                kb.tt(taA, r_f, bcp("r_k"), ALU.mult, eng="pool")
                kb.tt(rkb, taA, kmodA, ALU.mult)
                for pc in range(4):
                    ps = kb.psum()
                    kb.mm(ps[:, 0:TB], bones, rkb[:, pc, :])
                    kb.tt(bonblk[:, pc, :], ps[:, 0:TB], vT[:, pc, :], ALU.mult)
            csA = T4("cs")
            fl = lambda t_: t_.re("p a t -> p (a t)")
            kb.scan(fl(csA), cmask[:, 0:4 * TB], fl(sgA))
            if dirn == 1:
                c3 = fl(csA).re("p (n c) -> p n c", c=CH)
                l3 = fl(sgA).re("p (n c) -> p n c", c=CH)
                tot = kb.rot("r_tot", [128, 4 * NCH, 1], F32, n=1)
                kb.copy(tot, c3[:, :, CH - 1:CH], eng="pool")
                kb.tt(c3, l3, c3, ALU.subtract)
                kb.tt(c3, c3, tot.bc([128, 4 * NCH, CH]), ALU.add)
            csmA = T4("csm")
            kb.tt(csmA, csA, sgA, ALU.subtract, eng="pool")
            GA, GiA, GpA = T4("G"), csA, csmA
            kb.act(GA, csA, AF.Exp, scale=-LW)
            kb.act(GiA, csA, AF.Exp, scale=LW)
            kb.act(GpA, csmA, AF.Exp, scale=-LW)
            v4 = lambda t_: t_.re("p a (n c) -> p a n c", c=CH)
            kb.tt(AR[:, :, :, 64:128], v4(r_f), v4(GA), ALU.mult)
            kb.stt(AR[:, :, :, 0:64], v4(kkA), -1.0, v4(GpA), ALU.mult, ALU.mult)
            kb.tt(ktT, kmodA, GiA, ALU.mult)
            kb.tt(btT, betaA, GiA, ALU.mult, eng="pool")
            edge = CH - 1 if dirn == 0 else 0
            kb.copy(gC.re("p a (n o) -> p a n o", o=1), v4(GA)[:, :, :, edge:edge + 1], eng="pool")
            yblk = kb.rot("r_yblk", [128, 4, TB], F32, n=1)
            chunks = range(NCH) if dirn == 0 else range(NCH - 1, -1, -1)
            H8 = [(h, slice((h % 2) * 64, (h % 2) * 64 + 64), h // 2) for h in range(8)]
            chl = list(chunks)
            ccs = {ch: slice(ch * CH, (ch + 1) * CH) for ch in chl}
            L = {ch: {} for ch in chl}
            m1b = m1.re("p a c -> p (a c)").re("p (o x) -> p o x", o=1).bc([128, 4, 128])
            for ch in chl:
                cc = ccs[ch]
                psT = kb.psum().bitcast(BF16)
                srcs = (lambda p_: ktT[:, p_, cc], lambda p_: AR[:, p_, ch, 0:64], lambda p_: btT[:, p_, cc], lambda p_: vT[:, p_, cc])
                for ti, sf in enumerate(srcs):
                    for h, hp, pr in H8:
                        kb.tr(psT[hp, (ti * 4 + pr) * 64:(ti * 4 + pr) * 64 + 64], sf(pr)[hp, :], ident[hp, hp])
                tm = kb.rot("r_tm%d" % ch, [128, 4, 4, 64], BF16, n=1)
                kb.copy(tm, psT[:, 0:1024].re("p (a b c) -> p a b c", a=4, b=4), eng="act")
                L[ch]["tm"] = tm
            for ch in chl:
                cc = ccs[ch]
                ps1 = kb.psum()
                ps2 = kb.psum()
                ps3 = kb.psum()
                for h, hp, pr in H8:
                    kb.mm(ps1[hp, pr * 128:pr * 128 + 128], btT[hp, pr, cc], AR[hp, pr, ch, :])
                for h, hp, pr in H8:
                    kb.mm(ps2[hp, pr * 128:pr * 128 + 128], ktT[hp, pr, cc], AR[hp, pr, ch, :])
                for h, hp, pr in H8:
                    kb.mm(ps3[hp, pr * 64:pr * 64 + 64], AR[hp, pr, ch, 0:64], btT[hp, pr, cc])
                NA = kb.rot("r_NA%d" % ch, [128, 4, 128], BF16, n=1)
                MA = kb.rot("r_MA%d" % ch, [128, 4, 128], BF16, n=1)
                Nn = kb.rot("r_Nn%d" % ch, [128, 4, 64], BF16, n=1)
                kb.tt(NA, ps1[:, 0:512].re("p (a c) -> p a c", a=4), m1b, ALU.mult)
                kb.tt(MA, ps2[:, 0:512].re("p (a c) -> p a c", a=4), m1b, ALU.mult)
                kb.tt(Nn, ps3[:, 0:256].re("p (a c) -> p a c", a=4), mN.bc([128, 4, 64]), ALU.mult)
                L[ch].update(NA=NA, MA=MA, Nn=Nn)
            for ch in chl:
                psz = kb.psum()
                MA, vTM = L[ch]["MA"], L[ch]["tm"][:, 3]
                for h, hp, pr in H8:
                    kb.mm(psz[hp, pr * 64:pr * 64 + 64], MA[hp, pr, 0:64], vTM[hp, pr, :])
                Z0 = kb.rot("r_Z0%d" % ch, [128, 4, 64], BF16, n=1)
                kb.copy(Z0, psz[:, 0:256].re("p (a c) -> p a c", a=4), eng="act")
                Y = kb.rot("r_Y%d" % ch, [128, 4, 64], BF16, n=2)
                kb.tt(Y, L[ch]["NA"][:, :, 0:64], eye.bc([128, 4, 64]), ALU.add, eng="pool")
                L[ch].update(Z0=Z0, Y=Y, P=(lambda N_: (lambda p_: N_[:, p_, :]))(L[ch]["Nn"]),
                             PT=(lambda N_: (lambda p_: N_[:, p_, 0:64]))(L[ch]["NA"]))
            for j in range(1, 6):
                for ch in chl:
                    Pj, PTj = L[ch]["P"], L[ch]["PT"]
                    psq = kb.psum()
                    for h, hp, pr in H8:
                        kb.mm(psq[hp, (pr * 2) * 64:(pr * 2) * 64 + 64], PTj(pr)[hp, :], Pj(pr)[hp, :])
                        if j < 5:
                            kb.mm(psq[hp, (pr * 2 + 1) * 64:(pr * 2 + 1) * 64 + 64], Pj(pr)[hp, :], PTj(pr)[hp, :])
                    PP = kb.rot("r_PP%d" % ch, [128, 4, 2, 64], BF16, n=2)
                    if j < 5:
                        kb.copy(PP, psq[:, 0:512].re("p (a b c) -> p a b c", a=4, b=2), eng="act")
                    else:
                        kb.copy(PP[:, :, 0, :], psq[:, 0:512].re("p (a b c) -> p a b c", a=4, b=2)[:, :, 0, :], eng="act")
                    L[ch]["P"] = (lambda PP_: (lambda p_: PP_[:, p_, 0, :]))(PP)
                    L[ch]["PT"] = (lambda PP_: (lambda p_: PP_[:, p_, 1, :]))(PP)
                for ch in chl:
                    Pj, Y = L[ch]["P"], L[ch]["Y"]
                    psy = kb.psum()
                    for h, hp, pr in H8:
                        kb.mm(psy[hp, pr * 64:pr * 64 + 64], Pj(pr)[hp, :], Y[hp, pr, :])
                    Yn = kb.rot("r_Y%d" % ch, [128, 4, 64], BF16, n=2)
                    kb.tt(Yn, psy[:, 0:256].re("p (a c) -> p a c", a=4), Y, ALU.add)
                    L[ch]["Y"] = Yn
            for ch in chl:
                TTt, aTM = L[ch]["Y"], L[ch]["tm"][:, 1]
                psw = kb.psum()
                for h, hp, pr in H8:
                    kb.mm(psw[hp, pr * 64:pr * 64 + 64], aTM[hp, pr, :], TTt[hp, pr, :])
                WTg = kb.rot("r_WT%d" % ch, [128, 4, 64], BF16, n=1)
                kb.copy(WTg, psw[:, 0:256].re("p (a c) -> p a c", a=4), eng="act")
                L[ch]["WT"] = WTg
            for ch in chl:
                cc = ccs[ch]
                tm, NA, MA, Z0, TTt, WTg = L[ch]["tm"], L[ch]["NA"], L[ch]["MA"], L[ch]["Z0"], L[ch]["Y"], L[ch]["WT"]
                kTM, bTM, vTM = tm[:, 0], tm[:, 2], tm[:, 3]
                psu = kb.psum()
                for h, hp, pr in H8:
                    kb.mm(psu[hp, pr * 64:pr * 64 + 64], TTt[hp, pr, :], Z0[hp, pr, :], start=True, stop=False)
                    kb.mm(psu[hp, pr * 64:pr * 64 + 64], WTg[hp, pr, :], Hb[hp, pr, :], start=False, stop=True)
                Ug = kb.rot("r_U", [128, 4, 64], BF16, n=2)
                kb.copy(Ug, psu[:, 0:256].re("p (a c) -> p a c", a=4), eng="act")
                psH = kb.psum()
                for h, hp, pr in H8:
                    o_ = psH[hp, pr * 64:pr * 64 + 64]
                    kb.mm(o_, kTM[hp, pr, :], vTM[hp, pr, :], start=True, stop=False)
                    kb.mm(o_, bTM[hp, pr, :], Ug[hp, pr, :], start=False, stop=True)
                kb.tt(H32, psH[:, 0:256].re("p (a c) -> p a c", a=4), H32, ALU.add)
                psY = kb.psum()
                for h, hp, pr in H8:
                    o_ = psY[hp, pr * 64:pr * 64 + 64]
                    kb.mm(o_, Hb[hp, pr, :], AR[hp, pr, ch, 64:128], start=True, stop=False)
                    kb.mm(o_, Ug[hp, pr, :], NA[hp, pr, 64:128], start=False, stop=False)
                    kb.mm(o_, vTM[hp, pr, :], MA[hp, pr, 64:128], start=False, stop=True)
                kb.tt(Hb, H32, gC[:, :, ch:ch + 1].bc([128, 4, 64]), ALU.mult, eng="pool")
                kb.tt(H32, H32, gC[:, :, ch:ch + 1].bc([128, 4, 64]), ALU.mult, eng="pool")
                kb.copy(yblk[:, :, cc], psY[:, 0:256].re("p (a c) -> p a c", a=4), eng="act")
            kb.dma(SV("yB", s, dirn, yB_s[s, dirn, :, :, t0:t0 + TB]), yblk, q="act")
            if dirn == 0:
                kb.dma(SV("bon", s, 0, bon_s[s, :, :, t0:t0 + TB]), bonblk, q="act")

    def compute_hT(s):
        xs = xT[s].rearrange("(kc p) t -> p kc t", p=128)
        kb.memset(hT[:, :, 0:1], 0.0)
        kb.memset(hT[:, :, T + 1:T + 2], 0.0)
        for b in range(NB):
            t0 = b * TB
            xblk = kb.rot("xblk", [128, 8, TB], F32, n=2)
            kb.dma(xblk, DV(xs[:, :, t0:t0 + TB]), q="sp")
            sq = kb.rot("sq", [128, 8, TB], BF16, n=1)
            rstd = kb.rot("rstd", [128, TB], F32, n=2)
            kb.act(sq, xblk, AF.Square)
            ps = kb.psum()
            for kc in range(8):
                kb.mm(ps[:, 0:TB], ones, sq[:, kc, :], start=(kc == 0), stop=(kc == 7))
            rsqrt(rstd, ps[:, 0:TB], float(D) * 1e-6)
            for kc in range(8):
                kb.stt(hT[:, kc, 1 + t0:1 + t0 + TB], xblk[:, kc, :], 32.0, rstd, ALU.mult, ALU.mult)

    def final_phase(s, preW=None):
        xs = xT[s].rearrange("(kc p) t -> p kc t", p=128)
        Wgp = kb.rot("f_Wgp", [128, 4, D], BF16, n=1)
        Wrp = kb.rot("f_Wrp", [128, 4, D], BF16, n=1)
        Wo = kb.rot("f_Wo", [128, 8, D], BF16, n=1)
        kb.dma(Wgp, DV(glap_s[:, :, :]), q="sp")
        kb.dma(Wrp, DV(rwp_s[:, :, :]), q="sp")
        kb.dma(Wo, DV(wout_s[:, :, :]), q="sp")
        x1v = x1_s[s].rearrange("(kc p) t -> p kc t", p=128)
        for b in range(NB):
            t0 = b * TB
            hb = lambda kc: hT[:, kc, 1 + t0:1 + t0 + TB]
            tsl = slice(t0, t0 + TB)
            of_ = kb.rot("f_a", [128, 4, TB], F32, n=2)
            ob_ = kb.rot("f_b", [128, 4, TB], F32, n=2)
            kb.dma(of_, SV("oA", s, 0, oA_s[s, 0, :, :, tsl]), q="sp")
            kb.dma(ob_, SV("oA", s, 1, oA_s[s, 1, :, :, tsl]), q="sp")
            yf = kb.rot("f_a", [128, 4, TB], F32, n=2)
            yb_ = kb.rot("f_b", [128, 4, TB], F32, n=2)
            kb.dma(yf, SV("yB", s, 0, yB_s[s, 0, :, :, tsl]), q="sp")
            kb.dma(yb_, SV("yB", s, 1, yB_s[s, 1, :, :, tsl]), q="sp")
            bon = kb.rot("f_bon", [128, 4, TB], F32, n=2)
            kb.dma(bon, SV("bon", s, 0, bon_s[s, :, :, tsl]), q="sp")
            sglb = kb.rot("f_sgl", [128, TB], BF16, n=2)
            kb.dma(sglb, SV("sgl", s, 0, sgl_s[s, :, tsl]), q="sp")
            kb.tt(of_, of_, ob_, ALU.add, eng="pool")
            sqg = kb.rot("f_sq", [128, 4, TB], BF16, n=1)
            kb.tt(sqg, of_, of_, ALU.mult, eng="pool")
            Wog = preW if (b == 0 and preW is not None) else loadw(1024, 512)
            ofin = kb.rot("f_ofin", [128, 4, TB], BF16, n=1)
            ons = []
            for h in range(4):
                ps = kb.psum()
                kb.mm(ps[:, 0:TB], ones, sqg[:, h, :])
                rs = kb.rot("f_rs", [128, TB], F32, n=2)
                rsqrt(rs, ps[:, 0:TB], 128.0 * 1e-5)
                on = kb.rot("f_on", [128, TB], F32, n=4)
                kb.stt(on, of_[:, h, :], pv("gng", h), rs, ALU.mult, ALU.mult)
                ons.append(on)
            kb.tt(yf, yf, yb_, ALU.add, eng="pool")
            y16 = kb.rot("f_y16", [128, 4, TB], BF16, n=1)
            kb.copy(y16, yf, eng="pool")
            ysq = kb.rot("f_sq", [128, 4, TB], BF16, n=1)
            kb.tt(ysq, yf, yf, ALU.mult, eng="pool")
            orw = kb.rot("f_orw", [128, 4, TB], BF16, n=1)
            for pc in range(4):
                psm = kb.psum()
                kb.mm(psm[:, 0:TB], bones, y16[:, pc, :])
                psq = kb.psum()
                kb.mm(psq[:, 0:TB], bones, ysq[:, pc, :])
                mean = kb.rot("f_mean", [128, TB], F32, n=2)
                kb.act(mean, psm[:, 0:TB], AF.Copy, scale=1.0 / 64.0)
                msq = kb.rot("f_msq", [128, TB], F32, n=2)
                kb.tt(msq, mean, mean, ALU.mult, eng="pool")
                var = kb.rot("f_var", [128, TB], F32, n=2)
                kb.stt(var, psq[:, 0:TB], 1.0 / 64.0, msq, ALU.mult, ALU.subtract)
                rsqrt(var, var, 64.0 * 1e-5)
                yc = kb.rot("f_yc", [128, TB], F32, n=2)
                kb.tt(yc, yf[:, pc, :], mean, ALU.subtract, eng="pool")
                kb.tt(yc, yc, var, ALU.mult, eng="pool")
                kb.ts(yc, yc, pv("ln_w", pc), pv("ln_b", pc), ALU.mult, ALU.add)
                kb.tt(yc, yc, bon[:, pc, :], ALU.add, eng="pool")
                psg = kb.psum()
                kb.mm(psg[:, 0:TB], g2b[:, pc * 128:pc * 128 + 128], sglb)
                kb.tt(orw[:, pc, :], psg[:, 0:TB], yc, ALU.mult)
            for h in range(4):
                on = ons[h]
                ps2 = kb.psum()
                for kc in range(8):
                    kb.mm(ps2[:, 0:TB], Wog[:, kc, h * 128:h * 128 + 128], hb(kc), start=(kc == 0), stop=(kc == 7))
                sg = kb.rot("f_sg", [128, TB], F32, n=2)
                kb.act(sg, ps2[:, 0:TB], AF.Sigmoid)
                kb.tt(sg, ps2[:, 0:TB], sg, ALU.mult)
                kb.stt(ofin[:, h, :], on, float(np.sqrt(128.0)), sg, ALU.mult, ALU.mult)
            mrg = kb.rot("f_mrg", [128, 8, TB], F32, n=1)
            mrgb = kb.rot("f_mrgb", [128, 8, TB], BF16, n=1)
            for bi, (pview, feat, gc0) in enumerate(((Wgp, ofin, 3360), (Wrp, orw, 3360 + 1024))):
                for hf in range(2):
                    Wg = loadw(gc0 + hf * 512, 512)
                    for j in range(4):
                        oc = hf * 4 + j
                        psA = kb.psum()
                        for kc in range(4):
                            kb.mm(psA[:, 0:TB], pview[:, kc, oc * 128:oc * 128 + 128], feat[:, kc, :], start=(kc == 0), stop=(kc == 3))
                        psG = kb.psum()
                        for kc in range(8):
                            kb.mm(psG[:, 0:TB], Wg[:, kc, j * 128:j * 128 + 128], hb(kc), start=(kc == 0), stop=(kc == 7))
                        sgt = kb.rot("f_sg", [128, TB], F32, n=2)
                        kb.act(sgt, psG[:, 0:TB], AF.Sigmoid)
                        if bi == 0:
                            kb.tt(mrg[:, oc, :], psA[:, 0:TB], sgt, ALU.mult)
                        else:
                            kb.tt(sgt, psA[:, 0:TB], sgt, ALU.mult)
                            kb.tt(mrgb[:, oc, :], mrg[:, oc, :], sgt, ALU.add, eng="pool")
            xblk = kb.rot("xblk", [128, 8, TB], F32, n=2)
            kb.dma(xblk, DV(xs[:, :, tsl]), q="sp")
            for hf in range(2):
                for j in range(4):
                    oc = hf * 4 + j
                    ps = kb.psum()
                    for kc in range(8):
                        kb.mm(ps[:, 0:TB], Wo[:, kc, oc * 128:oc * 128 + 128], mrgb[:, kc, :], start=(kc == 0), stop=(kc == 7))
                    kb.tt(xblk[:, oc, :], ps[:, 0:TB], xblk[:, oc, :], ALU.add)
            kb.dma(SV("x1", s, 0, x1v[:, :, tsl]), xblk, q="act")

    alias_tiles = {}

    def ffn_phase(s):
        x1v = x1_s[s].rearrange("(kc p) t -> p kc t", p=128)
        ov = outT[s].rearrange("(kc p) t -> p kc t", p=128)
        hid = kb.rot("n_hid", [128, 22, TBF], BF16, n=1)
        kb.memset(hid[:, 21, :], 0.0)
        psh = kb.banks[7]
        if hT_bytes >= 4 * 8192:
            if "ffn_w" not in alias_tiles:
                alias_tiles["ffn_w"] = [kb.view_at("n_w%d" % i, hT_off + i * 8192, [128, 8, 512], BF16) for i in range(4)]
            fw = alias_tiles["ffn_w"]
        else:
            fw = [kb.rot("n_wsm", [128, 8, 512], BF16, n=4) for _ in range(4)]
        fcnt = [0]

        def floadw(c0, ncols):
            wb = fw[fcnt[0] % 4]
            fcnt[0] += 1
            kb.dma(wb[:, :, 0:ncols], DV(fup_s[:, :, c0:c0 + ncols]), q="sp")
            return wb

        def load_x1h(bb):
            tt0 = bb * TBF
            xh = kb.rot("n_x1h", [128, 8, TBF + 2], F32, n=2)
            kb.memset(xh[:, :, 0:1], 0.0)
            kb.memset(xh[:, :, TBF + 1:TBF + 2], 0.0)
            lo, hi = max(tt0 - 1, 0), min(tt0 + TBF + 1, T)
            kb.dma(xh[:, :, lo - (tt0 - 1):hi - (tt0 - 1)], SV("x1", s, 0, x1v[:, :, lo:hi]), q="sp")
            return xh

        nxt = load_x1h(0)
        for b in range(NBF):
            t0 = b * TBF
            x1h = nxt
            sq = kb.rot("n_sq", [128, 8, TBF + 2], BF16, n=1)
            kb.act(sq, x1h, AF.Square)
            ps = kb.psum()
            for kc in range(8):
                kb.mm(ps[:, 0:TBF], ones, sq[:, kc, 1:TBF + 1], start=(kc == 0), stop=(kc == 7))
            for kc in range(8):
                kb.mm(psh[:, 0:2], ones, sq[:, kc, 0:TBF + 2:TBF + 1], start=(kc == 0), stop=(kc == 7))
            rstd = kb.rot("n_rstd", [128, TBF + 2], F32, n=1)
            rsqrt(rstd[:, 1:TBF + 1], ps[:, 0:TBF], float(D) * 1e-6)
            rsqrt(rstd[:, 0:TBF + 2:TBF + 1], psh[:, 0:2], float(D) * 1e-6)
            h2T = kb.rot("n_h2T", [128, 8, TBF + 2], BF16, n=1)
            for kc in range(8):
                kb.stt(h2T[:, kc, :], x1h[:, kc, :], 32.0, rstd, ALU.mult, ALU.mult)
            if b + 1 < NBF:
                nxt = load_x1h(b + 1)
            pend = [None]
            for j0 in range(0, 22, 4):
                jn = min(4, 22 - j0)
                ncols = sum(128 if j < 21 else 64 for j in range(j0, j0 + jn))
                Wg4 = floadw(j0 * 128, ncols)
                Wv4 = floadw(DFF + j0 * 128, ncols)
                for jj in range(jn):
                    j = j0 + jj
                    nj = 128 if j < 21 else 64
                    cres = []
                    for gv, Wt in ((0, Wg4), (1, Wv4)):
                        cwn, cbn = ("cwg", "cbg") if gv == 0 else ("cwv", "cbv")
                        ps = kb.psum()
                        for kc in range(8):
                            kb.mm(ps[0:nj, 0:TBF], Wt[:, kc, jj * 128:jj * 128 + nj], h2T[:, kc, 1:TBF + 1], start=(kc == 0), stop=(kc == 7))
                        hc = slice(4 + 2 * gv, 6 + 2 * gv)
                        for kc in range(8):
                            kb.mm(psh[0:nj, hc], Wt[:, kc, jj * 128:jj * 128 + nj], h2T[:, kc, 0:TBF + 2:TBF + 1], start=(kc == 0), stop=(kc == 7))
                        U = kb.rot("n_U", [128, TBF + 2], F32, n=2)
                        kb.copy(U[0:nj, 1:TBF + 1], ps[0:nj, 0:TBF], eng="act")
                        kb.copy(U[0:nj, 0:TBF + 2:TBF + 1], psh[0:nj, hc], eng="act")
                        c = kb.rot("n_c%d" % gv, [128, TBF], F32, n=3)
                        kb.ts(c[0:nj, :], U[0:nj, 1:TBF + 1], pv(cwn, 22 + j)[0:nj, :], pv(cbn, j)[0:nj, :], ALU.mult, ALU.add, eng="pool")
                        kb.stt(c[0:nj, :], U[0:nj, 0:TBF], pv(cwn, j)[0:nj, :], c[0:nj, :], ALU.mult, ALU.add)
                        kb.stt(c[0:nj, :], U[0:nj, 2:TBF + 2], pv(cwn, 44 + j)[0:nj, :], c[0:nj, :], ALU.mult, ALU.add)
                        cres.append(c)
                    if pend[0] is not None:
                        pend[0]()

                    def fin(cg=cres[0], cv=cres[1], nj=nj, j=j):
                        kb.act(cg[0:nj, :], cg[0:nj, :], AF.Silu)
                        kb.tt(hid[0:nj, j, :], cg[0:nj, :], cv[0:nj, :], ALU.mult, eng="pool")
                    pend[0] = fin
            pend[0]()
            pend[0] = None
            x2 = x1h[:, :, 1:TBF + 1]
            for oc in range(8):
                if oc % 2 == 0:
                    Wdh = kb.rot("n_wd", [128, 22, 256], BF16, n=2)
                    kb.dma(Wdh, DV(wd_s[:, :, (oc // 2) * 256:(oc // 2) * 256 + 256]), q="sp")
                ps = kb.psum()
                for j in range(22):
                    kb.mm(ps[:, 0:TBF], Wdh[:, j, (oc % 2) * 128:(oc % 2) * 128 + 128], hid[:, j, :], start=(j == 0), stop=(j == 21))
                kb.tt(x2[:, oc, :], ps[:, 0:TBF], x2[:, oc, :], ALU.add)
            sq2 = kb.rot("n_sq", [128, 8, TBF + 2], BF16, n=1)
            kb.act(sq2[:, :, 0:TBF], x2, AF.Square)
            ps = kb.psum()
            for kc in range(8):
                kb.mm(ps[:, 0:TBF], ones, sq2[:, kc, 0:TBF], start=(kc == 0), stop=(kc == 7))
            rstd2 = kb.rot("n_rstd", [128, TBF + 2], F32, n=1)
            rsqrt(rstd2[:, 0:TBF], ps[:, 0:TBF], float(D) * 1e-6)
            for oc in range(8):
                kb.stt(x2[:, oc, :], x2[:, oc, :], gf32[:, oc:oc + 1], rstd2[:, 0:TBF], ALU.mult, ALU.mult)
            kb.dma(DV(ov[:, :, t0:t0 + TBF]), x2, q="act")

    kb.set_phase("prep")
    precast(w_in_v, 8, NPROJ, Win_s, "g1")
    precast(glap_v, 4, D, glap_s, None)
    precast(rwp_v, 4, D, rwp_s, None)
    precast(wout_v, 8, D, wout_s, None)
    precast(fup_v, 8, 2 * DFF, fup_s, "g2n")
    load_Wd()
    for s in range(NSEQ):
        kb.set_phase("hT")
        pre = loadw(0, 512)
        compute_hT(s)
        gla_sweep(s, 0, pre)
        gla_sweep(s, 1)
        pre = loadw(R0, 512)
        kb.set_phase("rwkv")
        rwkv_sweep(s, 0, pre)
        rwkv_sweep(s, 1)
        pre = loadw(1024, 512)
        kb.set_phase("final")
        final_phase(s, pre)
        kb.set_phase("ffn")
        ffn_phase(s)
    kb.P.emit()
    return nc


def pack_params(inp):
    pvm = np.zeros((128, PV_W), np.float32)

    def put(name, vec):
        o, w = PV_OFF[name]
        pvm[:, o:o + w] = _fm(vec, w)

    put("g1", inp["norm1_g"][0])
    put("g2n", inp["norm2_g"][0])
    put("gf", inp["norm_f_g"])
    put("ba_f", inp["gla_ba_f"][0])
    put("ba_b", inp["gla_ba_b"][0])
    put("gng", inp["gla_norm_g"][0])
    put("mu_p", inp["rwkv_mu_prev"][0])
    put("mu_n", inp["rwkv_mu_next"][0])
    put("w0_f", inp["rwkv_w0_f"][0])
    put("w0_b", inp["rwkv_w0_b"][0])
    put("a0", inp["rwkv_a0"][0])
    put("k_k", inp["rwkv_k_k"][0])
    put("k_a", inp["rwkv_k_a"][0])
    put("r_k", np.asarray(inp["rwkv_r_k"][0]).reshape(-1))
    put("ln_w", inp["rwkv_ln_w"][0])
    put("ln_b", inp["rwkv_ln_b"][0])
    cw = np.asarray(inp["ffn_conv_w"][0], np.float32)
    cb = np.asarray(inp["ffn_conv_b"][0], np.float32)

    def chunked(v):
        pad = np.zeros(22 * 128, np.float32)
        pad[:DFF] = v
        return pad.reshape(22, 128).T

    for nm, off in (("cwg", 0), ("cwv", DFF)):
        o, w = PV_OFF[nm]
        for tap in range(3):
            pvm[:, o + tap * 22:o + tap * 22 + 22] = chunked(cw[tap, off:off + DFF])
    for nm, off in (("cbg", 0), ("cbv", DFF)):
        o, w = PV_OFF[nm]
        pvm[:, o:o + 22] = chunked(cb[off:off + DFF])
    return pvm


_CACHE = {}


def run(inputs, n_cores, nseq, T):
    inp = {k: np.asarray(v) for k, v in inputs.items()}
    x = inp["x"]
    key = (n_cores, nseq, T)
    if key not in _CACHE:
        nc = bass.Bass("TRN2", target_bir_lowering=False)
        build(nc, nseq, T)
        _CACHE[key] = nc
    nc = _CACHE[key]
    shared = dict(make_consts())
    shared["pvec"] = pack_params(inp)
    f32c = lambda a: np.ascontiguousarray(np.asarray(a, np.float32))
    shared.update({
        "w_in": f32c(inp["w_in"][0]), "wa2_f": f32c(inp["gla_wa2_f"][0]), "wa2_b": f32c(inp["gla_wa2_b"][0]),
        "w2_f": f32c(inp["rwkv_w2_f"][0]), "w2_b": f32c(inp["rwkv_w2_b"][0]), "a2": f32c(inp["rwkv_a2"][0]),
        "g2": f32c(inp["rwkv_g2"][0]), "gla_proj": f32c(inp["gla_proj"][0]), "rwkv_proj": f32c(inp["rwkv_proj"][0]),
        "w_out": f32c(inp["w_out"][0]), "ffn_up": f32c(inp["ffn_up"][0]), "ffn_down": f32c(inp["ffn_down"][0]),
    })
    in_maps = []
    for c in range(n_cores):
        xs = x[c * nseq:(c + 1) * nseq]
        m = dict(shared)
        m["xT"] = np.ascontiguousarray(xs.transpose(0, 2, 1))
        in_maps.append(m)
    res = run_bass_kernel_spmd(nc, in_maps, core_ids=list(range(n_cores)))
    outs = [np.asarray(r["outT"]).transpose(0, 2, 1) for r in res.results]
    return np.ascontiguousarray(np.concatenate(outs, axis=0)).astype(np.float32)


def kernel(**inputs):
    x = np.asarray(inputs["x"])
    B, T, _ = x.shape
    return run(inputs, N_CORES, B // N_CORES, T)
```

```python
import numpy as np
import ml_dtypes
import concourse.bass as bass
import concourse.mybir as mybir
from concourse.bass_utils import run_bass_kernel_spmd

F32 = mybir.dt.float32
BF16 = mybir.dt.bfloat16
AF = mybir.ActivationFunctionType
ALU = mybir.AluOpType

D = 1024
NPROJ = 5408
DFF = 2752
SEM_EPOCH = 12000
N_CORES = 8
KT_ENG = "dve"
STOP_AFTER = None
RW_STOP = 99
PM_MODE = 2


class Res:
    __slots__ = ("name", "writers", "readers", "psum")

    def __init__(self, name, psum=False):
        self.name = name
        self.writers = []
        self.readers = []
        self.psum = psum


class Op:
    __slots__ = ("eng", "fn", "deps", "is_dma", "done", "need_inc", "slot", "idx")

    def __init__(self, eng, fn, is_dma, slot):
        self.eng = eng
        self.fn = fn
        self.deps = []
        self.is_dma = is_dma
        self.done = None
        self.need_inc = False
        self.slot = slot


class Prog:
    ENGS = ("pe", "act", "dve", "pool", "sp")

    def __init__(self, nc):
        self.nc = nc
        self.ops = {e: [] for e in self.ENGS}
        self.barrier_op = None
        self.last = {}
        self.dmas = []
        self.nrec = 0

    def op(self, eng, fn, reads=(), writes=(), dma_slot=None):
        is_dma = dma_slot is not None
        o = Op(eng, fn, is_dma, dma_slot)
        o.idx = self.nrec
        self.nrec += 1
        deps = []
        for r in reads:
            deps.extend(r.writers)
            if r.psum:
                deps.extend(x for x in r.readers if x.eng != eng)
        fills = []
        for r in writes:
            same_fill = (is_dma and r.writers and not r.readers
                         and all(w.is_dma and w.slot is dma_slot for w in r.writers))
            fills.append(same_fill)
            if not same_fill:
                deps.extend(r.writers)
            deps.extend(r.readers)
        seen = set()
        for d in deps:
            if d is o or id(d) in seen:
                continue
            seen.add(id(d))
            if d.eng == "pe" and eng == "pe" and not d.is_dma:
                continue
            o.deps.append(d)
            d.need_inc = True
        if self.barrier_op is not None and all(d is not self.barrier_op for d in o.deps):
            o.deps.append(self.barrier_op)
        self.last[eng] = o
        if is_dma:
            self.dmas.append(o)
        for r in reads:
            r.readers.append(o)
        for r, same_fill in zip(writes, fills):
            if same_fill:
                r.writers.append(o)
            else:
                r.writers = [o]
            r.readers = []
        self.ops[eng].append(o)
        return o

    def barrier(self, fn, eng="pool"):
        o = Op(eng, fn, False, None)
        o.idx = self.nrec
        self.nrec += 1
        seen = set()
        for d in list(self.last.values()) + self.dmas:
            if id(d) not in seen:
                seen.add(id(d))
                o.deps.append(d)
                d.need_inc = True
        o.need_inc = True
        self.ops[eng].append(o)
        self.barrier_op = o
        self.last = {eng: o}
        self.dmas = []

    def emit(self, final_wait_eng="sp"):
        nc = self.nc
        sems = {}
        keep = []

        def get_sem(key):
            if key not in sems:
                cm = nc.semaphore("s_%s_%s" % (key[0], key[1]))
                sems[key] = cm.__enter__()
                keep.append(cm)
            return sems[key]

        slot_ids, slot_cnt, keys = {}, {}, set()
        alld = sorted((o for e in self.ENGS for o in self.ops[e] if o.is_dma), key=lambda o: o.idx)
        for o in alld:
            sid = slot_ids.setdefault(id(o.slot), len(slot_ids))
            c = slot_cnt.get(sid, 0) + 16
            slot_cnt[sid] = c
            o.done = (("dma", sid), c)
            keys.add(o.done[0])
        for e in self.ENGS:
            cnt = 0
            for o in self.ops[e]:
                if o.is_dma:
                    pass
                elif o.need_inc:
                    cnt += 1
                    ep = (cnt - 1) // SEM_EPOCH
                    o.done = ((e, ep), cnt - ep * SEM_EPOCH)
                    keys.add(o.done[0])
        for k in sorted(keys, key=str):
            get_sem(k)
        final = dict(slot_cnt)
        engmap = {"pe": "tensor", "act": "scalar", "dve": "vector", "pool": "gpsimd", "sp": "sync"}
        with nc.Block() as block:
            for e in self.ENGS:
                ops = self.ops[e]
                is_final = (e == final_wait_eng)
                if not ops and not is_final:
                    continue

                def body(eng, ops=ops, is_final=is_final):
                    waited = {}
                    for o in ops:
                        need = {}
                        for d in o.deps:
                            k, v = d.done
                            if need.get(k, 0) < v:
                                need[k] = v
                        for k, v in need.items():
                            if waited.get(k, 0) >= v:
                                continue
                            waited[k] = v
                            eng.wait_ge(sems[k], v)
                        ins = o.fn(eng)
                        if o.is_dma:
                            ins.then_inc(sems[o.done[0]], 16)
                        elif o.need_inc:
                            ins.then_inc(sems[o.done[0]], 1)
                    if is_final:
                        for sid, c in final.items():
                            eng.wait_ge(sems[("dma", sid)], c)

                getattr(block, engmap[e])(body)


class V:
    __slots__ = ("ap", "res")

    def __init__(self, ap, res):
        self.ap = ap
        self.res = res

    def __getitem__(self, k):
        return V(self.ap[k], self.res)

    def re(self, pat, **kw):
        return V(self.ap.rearrange(pat, **kw), self.res)

    def bc(self, shape):
        return V(self.ap.to_broadcast(list(shape)), self.res)

    def bitcast(self, dt):
        return V(self.ap.bitcast(dt), self.res)


def _r(*vs):
    return [v.res for v in vs if isinstance(v, V) and v.res is not None]


def _a(x):
    return x.ap if isinstance(x, V) else x


class KB:
    def __init__(self, nc):
        self.nc = nc
        self.P = Prog(nc)
        self.rots = {}
        self.ARENA = 212000
        self.arena = nc.alloc_sbuf_tensor("arena", [128, self.ARENA // 4], F32)
        self.gptr = 0
        self.phase = None
        self.pptr = {}
        self.nps = 0
        self.banks = []
        for i in range(8):
            t = nc.alloc_psum_tensor("psb%d" % i, [128, 512], F32)
            self.banks.append(V(t[:], Res("psb%d" % i, psum=True)))
        self.uid = 0
        self.dummy = self.sb("dummy", [128, 8], F32)

    def sb(self, name, shape, dt):
        esz = 4 if dt == F32 else 2
        n = 1
        for d_ in shape[1:]:
            n *= d_
        nbytes = (n * esz + 31) // 32 * 32
        if self.phase is None:
            assert not self.pptr, "global allocation after first phase: " + name
            off = self.gptr
            self.gptr += nbytes
        else:
            off = self.pptr[self.phase]
            self.pptr[self.phase] += nbytes
        assert off + nbytes <= self.ARENA, ("SBUF arena overflow", name, self.phase, off, nbytes)
        self.last_off = off
        return self.view_at(name, off, shape, dt)

    def view_at(self, name, off, shape, dt):
        esz = 4 if dt == F32 else 2
        n = 1
        for d_ in shape[1:]:
            n *= d_
        nbytes = (n * esz + 31) // 32 * 32
        ap = self.arena[0:shape[0], off // 4:(off + nbytes) // 4]
        if dt != F32:
            ap = ap.bitcast(dt)
        ap = ap[:, 0:n]
        if len(shape) > 2:
            names = "abcdef"[:len(shape) - 1]
            pat = "p (%s) -> p %s" % (" ".join(names), " ".join(names))
            ap = ap.rearrange(pat, **{names[i]: shape[i + 1] for i in range(len(names))})
        return V(ap, Res(name))

    def set_phase(self, name):
        if name not in self.pptr:
            self.pptr[name] = self.gptr
        self.phase = name
        dummy = self.dummy
        self.P.barrier(lambda e: e.memset(dummy.ap, 0.0))

    def rot(self, tag, shape, dt, n=2):
        key = tag if (tag in self.rots or self.phase is None) else (self.phase, tag)
        if key not in self.rots:
            self.rots[key] = [[self.sb("%s_%d" % (tag, i), shape, dt) for i in range(n)], 0]
        lst = self.rots[key]
        v = lst[0][lst[1] % len(lst[0])]
        lst[1] += 1
        return v

    def psum(self):
        v = self.banks[self.nps % 7]
        self.nps += 1
        return v

    def dram(self, name, shape, dt, kind="Internal"):
        t = self.nc.dram_tensor(name, list(shape), dt, kind=kind)
        return t.ap()

    def mm(self, out, lhsT, rhs, start=True, stop=True):
        self.P.op("pe", lambda e: e.matmul(out.ap, lhsT=lhsT.ap, rhs=rhs.ap, start=start, stop=stop),
                  reads=_r(lhsT, rhs), writes=_r(out))

    def tr(self, out, in_, ident):
        self.P.op("pe", lambda e: e.transpose(out.ap, in_.ap, ident.ap), reads=_r(in_, ident), writes=_r(out))

    def act(self, out, in_, func, bias=0.0, scale=1.0, eng="act"):
        self.P.op(eng, lambda e: e.activation(out=out.ap, in_=in_.ap, func=func, bias=_a(bias), scale=_a(scale)),
                  reads=_r(in_, bias, scale), writes=_r(out))

    def tt(self, out, a, b, op, eng="dve"):
        self.P.op(eng, lambda e: e.tensor_tensor(out=out.ap, in0=a.ap, in1=b.ap, op=op), reads=_r(a, b), writes=_r(out))

    def ts(self, out, a, s1, s2, op0, op1=None, eng="dve"):
        if op1 is None:
            self.P.op(eng, lambda e: e.tensor_scalar(out=out.ap, in0=a.ap, scalar1=_a(s1), scalar2=None, op0=op0),
                      reads=_r(a, s1), writes=_r(out))
        else:
            self.P.op(eng, lambda e: e.tensor_scalar(out=out.ap, in0=a.ap, scalar1=_a(s1), scalar2=_a(s2), op0=op0, op1=op1),
                      reads=_r(a, s1, s2), writes=_r(out))

    def stt(self, out, in0, scalar, in1, op0, op1, eng="dve"):
        self.P.op(eng, lambda e: e.scalar_tensor_tensor(out=out.ap, in0=in0.ap, scalar=_a(scalar), in1=in1.ap, op0=op0, op1=op1),
                  reads=_r(in0, scalar, in1), writes=_r(out))

    def copy(self, out, in_, eng="act"):
        if eng == "act":
            self.P.op("act", lambda e: e.copy(out=out.ap, in_=in_.ap), reads=_r(in_), writes=_r(out))
        else:
            self.P.op(eng, lambda e: e.tensor_copy(out=out.ap, in_=in_.ap), reads=_r(in_), writes=_r(out))

    def scan(self, out, d0, d1):
        self.P.op("dve", lambda e: e.tensor_tensor_scan(out=out.ap, data0=d0.ap, data1=d1.ap, initial=0.0,
                                                        op0=ALU.mult, op1=ALU.add), reads=_r(d0, d1), writes=_r(out))

    def memset(self, out, val, eng="pool"):
        self.P.op(eng, lambda e: e.memset(out.ap, val), writes=_r(out))

    def dma(self, out, in_, q="sp", slow=False):
        o_sb = out.ap.space is not None and "DRAM" not in str(out.ap.space).upper() and "HBM" not in str(out.ap.space).upper()
        slot = out.res if o_sb else in_.res
        kw = {"allow_slow_non_contiguous": True} if slow else {}
        self.P.op(q, lambda e: e.dma_start(out=out.ap, in_=in_.ap, **kw), reads=_r(in_), writes=_r(out), dma_slot=slot)


def _fm(v, nch):
    return np.ascontiguousarray(np.asarray(v, np.float32).reshape(nch, 128).T)


PV_SPEC = [("g1", 8), ("g2n", 8), ("gf", 8), ("ba_f", 2), ("ba_b", 2), ("gng", 4), ("mu_p", 14), ("mu_n", 14),
           ("w0_f", 4), ("w0_b", 4), ("a0", 4), ("k_k", 4), ("k_a", 4), ("r_k", 4), ("ln_w", 4), ("ln_b", 4),
           ("cwg", 66), ("cwv", 66), ("cbg", 22), ("cbv", 22)]
PV_OFF = {}
_o = 0
for _n, _w in PV_SPEC:
    PV_OFF[_n] = (_o, _w)
    _o += _w
PV_W = _o


def make_consts():
    p = np.arange(128) % 64
    j = np.arange(64)
    su = (j[None, :] > p[:, None]).astype(np.float32)
    iu = (j[None, :] >= p[:, None]).astype(np.float32)
    sl = (j[None, :] < p[:, None]).astype(np.float32)
    il = (j[None, :] <= p[:, None]).astype(np.float32)
    eye = (j[None, :] == p[:, None]).astype(np.float32)
    c = {}
    c["ident"] = np.eye(128, dtype=np.float32).astype(ml_dtypes.bfloat16)
    c["ones"] = np.ones((128, 128), np.float32).astype(ml_dtypes.bfloat16)
    bo = np.zeros((128, 128), np.float32)
    bo[:64, :64] = 1
    bo[64:, 64:] = 1
    c["bones"] = bo.astype(ml_dtypes.bfloat16)
    cm = np.ones((128, 1024), np.float32)
    cm[:, ::64] = 0
    c["cmask"] = cm
    c["masks"] = np.ascontiguousarray(np.stack([su, iu, sl, il, eye], 1))
    return c


def build(nc, NSEQ, T):
    kb = KB(nc)
    TB = min(256, T)
    NB = T // TB
    TBF = min(512, T)
    NBF = T // TBF
    CH = 64
    NCH = TB // CH
    R0 = 1568
    EI = "ExternalInput"
    xT = kb.dram("xT", [NSEQ, D, T], F32, kind=EI)
    pvec_d = kb.dram("pvec", [128, PV_W], F32, kind=EI)
    ident_d = kb.dram("ident", [128, 128], BF16, kind=EI)
    ones_d = kb.dram("ones", [128, 128], BF16, kind=EI)
    bones_d = kb.dram("bones", [128, 128], BF16, kind=EI)
    cmask_d = kb.dram("cmask", [128, 1024], F32, kind=EI)
    masks_d = kb.dram("masks", [128, 5, 64], F32, kind=EI)
    w_in = kb.dram("w_in", [D, NPROJ], F32, kind=EI)
    wa2_d = [kb.dram("wa2_f", [16, 256], F32, kind=EI), kb.dram("wa2_b", [16, 256], F32, kind=EI)]
    w2_d = [kb.dram("w2_f", [64, 512], F32, kind=EI), kb.dram("w2_b", [64, 512], F32, kind=EI)]
    a2_d = kb.dram("a2", [64, 512], F32, kind=EI)
    g2_d = kb.dram("g2", [128, 512], F32, kind=EI)
    glap_d = kb.dram("gla_proj", [512, D], F32, kind=EI)
    rwp_d = kb.dram("rwkv_proj", [512, D], F32, kind=EI)
    wout_d = kb.dram("w_out", [D, D], F32, kind=EI)
    fup_d = kb.dram("ffn_up", [D, 2 * DFF], F32, kind=EI)
    fdn_d = kb.dram("ffn_down", [DFF, D], F32, kind=EI)
    outT = kb.dram("outT", [NSEQ, D, T], F32, kind="ExternalOutput")
    oA_s = kb.dram("oA_s", [NSEQ, 2, 128, 4, T], F32)
    yB_s = kb.dram("yB_s", [NSEQ, 2, 128, 4, T], F32)
    bon_s = kb.dram("bon_s", [NSEQ, 128, 4, T], F32)
    sgl_s = kb.dram("sgl_s", [NSEQ, 128, T], BF16)
    x1_s = kb.dram("x1_s", [NSEQ, D, T], F32)
    rf_s = kb.dram("rf_s", [NSEQ, 128, 4, T], F32)
    kf_s = kb.dram("kf_s", [NSEQ, 128, 4, T], F32)
    vt_s = kb.dram("vt_s", [NSEQ, 128, 4, T], BF16)
    wa_s = kb.dram("wa_s", [NSEQ, 128, T], F32)
    gq_s = kb.dram("gq_s", [NSEQ, 128, 2, T], F32)
    kk_s = kb.dram("kk_s", [NSEQ, 128, 4, T], F32)
    gv_s = kb.dram("gv_s", [NSEQ, 128, T // 64, 256], BF16)
    rv_s = kb.dram("rv_s", [NSEQ, 128, T // 64, 256], BF16)
    km_s = kb.dram("km_s", [NSEQ, 128, 4, T], F32)
    be_s = kb.dram("be_s", [NSEQ, 128, 4, T], F32)
    gk_s = kb.dram("gk_s", [NSEQ, 128, 2, T], F32)
    w_in_v = w_in.rearrange("(kc p) c -> p kc c", p=128)
    glap_v = glap_d.rearrange("(kc p) c -> p kc c", p=128)
    rwp_v = rwp_d.rearrange("(kc p) c -> p kc c", p=128)
    wout_v = wout_d.rearrange("(kc p) c -> p kc c", p=128)
    fup_v = fup_d.rearrange("(kc p) c -> p kc c", p=128)

    def DV(ap):
        return V(ap, None)

    scr = {}

    def SV(name, s, d_, ap):
        k = (name, s, d_)
        if k not in scr:
            scr[k] = Res("scr_%s_%d_%d" % k)
        return V(ap, scr[k])

    pvec = kb.sb("pvec", [128, PV_W], F32)
    ident = kb.sb("ident", [128, 128], BF16)
    ones = kb.sb("ones", [128, 128], BF16)
    bones = kb.sb("bones", [128, 128], BF16)
    cmask = kb.sb("cmask", [128, 1024], F32)
    masks = kb.sb("masks", [128, 5, 64], F32)
    for dst, src in ((pvec, pvec_d), (ident, ident_d), (ones, ones_d), (bones, bones_d), (cmask, cmask_d), (masks, masks_d)):
        kb.dma(dst, DV(src), q="sp")

    def pv(name, j=None):
        o, w = PV_OFF[name]
        if j is None:
            return pvec[:, o:o + w]
        return pvec[:, o + j:o + j + 1]

    hT = kb.sb("hT", [128, 8, T + 2], BF16)
    hT_off = kb.last_off
    hT_bytes = 8 * (T + 2) * 2
    kb.rot("wbf", [128, 8, 512], BF16, n=3)
    gf32 = kb.sb("gf32", [128, 8], F32)
    kb.ts(gf32, pv("gf"), 32.0, None, ALU.mult)

    def rsqrt(out, in_, add):
        kb.act(out, in_, AF.Ln, bias=add)
        kb.act(out, out, AF.Exp, scale=-0.5)

    Win_s = kb.dram("Win_s", [128, 8, NPROJ], BF16)
    glap_s = kb.dram("glap_s", [128, 4, D], BF16)
    rwp_s = kb.dram("rwp_s", [128, 4, D], BF16)
    wout_s = kb.dram("wout_s", [128, 8, D], BF16)
    fup_s = kb.dram("fup_s", [128, 8, 2 * DFF], BF16)

    def precast(view, kcn, ncols_total, dst_s, gname):
        for c0 in range(0, ncols_total, 512):
            n = min(512, ncols_total - c0)
            st = kb.rot("pst", [128, 8, 512], F32, n=2)
            kb.dma(st[:, 0:kcn, 0:n], DV(view[:, 0:kcn, c0:c0 + n]), q="sp")
            wb = kb.rot("pbf", [128, 8, 512], BF16, n=2)
            if gname is None:
                h_ = max(kcn // 2, 1)
                kb.copy(wb[:, 0:h_, 0:n], st[:, 0:h_, 0:n], eng="act")
                kb.copy(wb[:, h_:kcn, 0:n], st[:, h_:kcn, 0:n], eng="pool")
            else:
                for kc in range(kcn):
                    if kc % 3 == 0:
                        kb.act(wb[:, kc, 0:n], st[:, kc, 0:n], AF.Copy, scale=pv(gname, kc))
                    elif kc % 3 == 1:
                        kb.ts(wb[:, kc, 0:n], st[:, kc, 0:n], pv(gname, kc), None, ALU.mult)
                    else:
                        kb.ts(wb[:, kc, 0:n], st[:, kc, 0:n], pv(gname, kc), None, ALU.mult, eng="pool")
            kb.dma(DV(dst_s[:, 0:kcn, c0:c0 + n]), wb[:, 0:kcn, 0:n], q="act")

    def loadwx(view_s, kcn, c0, ncols):
        wb = kb.rot("wbf", [128, 8, 512], BF16, n=3)
        kb.dma(wb[:, 0:kcn, 0:ncols], DV(view_s[:, 0:kcn, c0:c0 + ncols]), q="sp")
        return wb

    def loadw(c0, ncols):
        return loadwx(Win_s, 8, c0, ncols)

    g2st = kb.sb("g2st", [128, 512], F32)
    kb.dma(g2st, DV(g2_d), q="sp")
    g2b = kb.sb("g2b", [128, 512], BF16)
    kb.copy(g2b, g2st, eng="pool")
    wd_s = kb.dram("wd_s", [128, 22, D], BF16)
    fdn_v = fdn_d[0:21 * 128, :].rearrange("(j p) c -> p j c", p=128)

    def load_Wd():
        Wd = kb.rot("p_Wd", [128, 22, D], BF16, n=1)
        kb.memset(Wd[:, 21, :], 0.0)
        for j0, jn in ((0, 8), (8, 8), (16, 5)):
            for hf in range(2):
                st = kb.rot("pst", [128, 8, 512], F32, n=2)
                kb.dma(st[:, 0:jn, :], DV(fdn_v[:, j0:j0 + jn, hf * 512:hf * 512 + 512]), q="sp")
                kb.copy(Wd[:, j0:j0 + jn, hf * 512:hf * 512 + 512], st[:, 0:jn, :], eng=("act" if hf == 0 else "pool"))
        for hf in range(2):
            st = kb.rot("pst", [128, 8, 512], F32, n=2)
            kb.dma(st[0:64, 0, :], DV(fdn_d[21 * 128:DFF, hf * 512:hf * 512 + 512]), q="sp")
            kb.copy(Wd[0:64, 21, hf * 512:hf * 512 + 512], st[0:64, 0, :], eng="act")
        kb.dma(DV(wd_s[:, :, :]), Wd, q="act")

    wa2 = []
    for d_ in range(2):
        st = kb.sb("wa2st%d" % d_, [16, 256], F32)
        kb.dma(st, DV(wa2_d[d_]), q="sp")
        wb = kb.sb("wa2b%d" % d_, [16, 256], BF16)
        kb.copy(wb, st, eng="pool")
        wa2.append(wb)
    nba = kb.sb("nba", [128, 4], F32)
    kb.ts(nba[:, 0:2], pv("ba_f"), -1.0, None, ALU.mult)
    kb.ts(nba[:, 2:4], pv("ba_b"), -1.0, None, ALU.mult)
    S32 = kb.sb("S32", [128, 2, 128], F32)
    Sb = kb.sb("Sb", [128, 2, 128], BF16)

    def hs(h):
        return slice((h % 2) * 64, (h % 2) * 64 + 64)

    def gla_sweep(s, dirn, preW=None):
        kb.memset(S32, 0.0)
        kb.memset(Sb, 0.0)
        mA = masks[:, 1:2, :] if dirn == 0 else masks[:, 3:4, :]
        blocks = range(NB) if dirn == 0 else range(NB - 1, -1, -1)
        for b in blocks:
            t0 = b * TB
            hblk = lambda kc: hT[:, kc, 1 + t0:1 + t0 + TB]
            q_f = kb.rot("q_f", [128, 2, TB], F32, n=2)
            k_f = kb.rot("k_f", [128, 2, TB], F32, n=2)
            if dirn == 0:
                Wqk = preW if (b == 0 and preW is not None) else loadw(0, 512)
                for i, dst in ((0, q_f), (1, k_f)):
                    for pc in range(2):
                        ps = kb.psum()
                        for kc in range(8):
                            kb.mm(ps[:, 0:TB], Wqk[:, kc, i * 256 + pc * 128:i * 256 + pc * 128 + 128], hblk(kc), start=(kc == 0), stop=(kc == 7))
                        kb.copy(dst[:, pc, :], ps[:, 0:TB], eng="act")
                kb.dma(SV("gq", s, 0, gq_s[s, :, :, t0:t0 + TB]), q_f, q="act")
                kb.dma(SV("gk", s, 0, gk_s[s, :, :, t0:t0 + TB]), k_f, q="act")
            else:
                kb.dma(q_f, SV("gq", s, 0, gq_s[s, :, :, t0:t0 + TB]), q="sp")
                kb.dma(k_f, SV("gk", s, 0, gk_s[s, :, :, t0:t0 + TB]), q="sp")
            Wa = loadw(1536 + 16 * dirn, 16)
            ps = kb.psum()
            for kc in range(8):
                kb.mm(ps[0:16, 0:TB], Wa[:, kc, 0:16], hblk(kc), start=(kc == 0), stop=(kc == 7))
            afT = kb.rot("afT", [16, TB], BF16, n=1)
            kb.copy(afT, ps[0:16, 0:TB], eng="act")
            cs = kb.rot("gcs", [128, 2, TB], F32, n=1)
            lg = kb.rot("glg", [128, 2, TB], F32, n=1)
            G = kb.rot("gG", [128, 2, TB], F32, n=1)
            Gi = kb.rot("gGi", [128, 2, TB], F32, n=1)
            qbT = kb.rot("qbT", [128, 2, TB], BF16, n=1)
            ktT = kb.rot("ktT", [128, 2, TB], BF16, n=1)
            gC = kb.rot("ggC", [128, 2, NCH], F32, n=1)
            for pc in range(2):
                ps = kb.psum()
                kb.mm(ps[:, 0:TB], wa2[dirn][0:16, pc * 128:pc * 128 + 128], afT[0:16, :])
                kb.act(lg[:, pc, :], ps[:, 0:TB], AF.Exp, bias=nba[:, 2 * dirn + pc:2 * dirn + pc + 1], scale=-1.0)
                kb.act(lg[:, pc, :], lg[:, pc, :], AF.Ln, bias=1.0)
                kb.scan(cs[:, pc, :], cmask[:, 0:TB], lg[:, pc, :])
                if dirn == 1:
                    c3 = cs[:, pc, :].re("p (n c) -> p n c", c=CH)
                    l3 = lg[:, pc, :].re("p (n c) -> p n c", c=CH)
                    tot = kb.rot("gtot", [128, NCH, 1], F32, n=1)
                    kb.copy(tot, c3[:, :, CH - 1:CH], eng="pool")
                    kb.tt(c3, l3, c3, ALU.subtract)
                    kb.tt(c3, c3, tot.bc([128, NCH, CH]), ALU.add)
                kb.act(G[:, pc, :], cs[:, pc, :], AF.Exp, scale=-1.0 / 16.0)
                kb.act(Gi[:, pc, :], cs[:, pc, :], AF.Exp, scale=1.0 / 16.0)
                kb.stt(qbT[:, pc, :], q_f[:, pc, :], 0.125, G[:, pc, :], ALU.mult, ALU.mult)
                kb.tt(ktT[:, pc, :], k_f[:, pc, :], Gi[:, pc, :], ALU.mult, eng="pool")
                g3 = G[:, pc, :].re("p (n c) -> p n c", c=CH)
                edge = CH - 1 if dirn == 0 else 0
                kb.copy(gC[:, pc, :].re("p (n o) -> p n o", o=1), g3[:, :, edge:edge + 1], eng="pool")
            vblk = kb.rot("gvblk", [128, NCH, 2, 128], BF16, n=2)
            gch = slice(b * NCH, (b + 1) * NCH)
            if dirn == 0:
                Wv = loadw(512, 512)
                Wv4 = Wv.re("p k (pr hf v) -> p k pr hf v", pr=2, hf=2)
            else:
                kb.dma(vblk.re("p n a v -> p n (a v)"), SV("gv", s, 0, gv_s[s, :, gch, :]), q="sp")
            oblk = kb.rot("oblk", [128, 4, TB], F32, n=1)
            chunks = range(NCH) if dirn == 0 else range(NCH - 1, -1, -1)
            for ch in chunks:
                cc = slice(ch * CH, (ch + 1) * CH)
                tc0 = 1 + t0 + ch * CH
                vTM = vblk[:, ch]
                if dirn == 0:
                    psV = kb.psum()
                    for half in range(2):
                        for kc in range(8):
                            kb.mm(psV[half * 64:half * 64 + 64, 0:256].re("p (pr v) -> p pr v", pr=2),
                                  hT[:, kc, tc0:tc0 + CH], Wv4[:, kc, :, half, :], start=(kc == 0), stop=(kc == 7))
                    kb.copy(vTM, psV[:, 0:256].re("p (pr v) -> p pr v", pr=2), eng="act")
                psT = kb.psum().bitcast(BF16)
                for h in range(4):
                    kb.tr(psT[hs(h), (h // 2) * 64:(h // 2) * 64 + 64], ktT[hs(h), h // 2, cc], ident[hs(h), hs(h)])
                kTM = kb.rot("gkTM", [128, 2, 64], BF16, n=2)
                kb.copy(kTM, psT[:, 0:128].re("p (pr k) -> p pr k", pr=2), eng="act")
                psA = kb.psum()
                for h in range(4):
                    kb.mm(psA[hs(h), (h // 2) * 64:(h // 2) * 64 + 64], ktT[hs(h), h // 2, cc], qbT[hs(h), h // 2, cc])
                ATg = kb.rot("gAT", [128, 2, 64], BF16, n=2)
                kb.tt(ATg, psA[:, 0:128].re("p (pr c) -> p pr c", pr=2), mA.bc([128, 2, 64]), ALU.mult)
                psO2 = [kb.psum(), kb.psum()]
                for h in range(4):
                    po = psO2[h % 2][:, (h // 2) * 64:(h // 2) * 64 + 64]
                    kb.mm(po, vTM[hs(h), h // 2, :], ATg[hs(h), h // 2, :], start=True, stop=False)
                    kb.mm(po, Sb[hs(h), h // 2, :], qbT[hs(h), h // 2, cc], start=False, stop=True)
                ob4 = oblk.re("p (pr hf) t -> p pr hf t", hf=2)
                for hf_ in range(2):
                    kb.copy(ob4[:, :, hf_, cc], psO2[hf_][:, 0:128].re("p (pr c) -> p pr c", pr=2), eng="act")
                psK = kb.psum()
                for h in range(4):
                    kb.mm(psK[hs(h), (h // 2) * 128:(h // 2) * 128 + 128], kTM[hs(h), h // 2, :], vTM[hs(h), h // 2, :])
                kb.tt(S32, psK[:, 0:256].re("p (pr v) -> p pr v", pr=2), S32, ALU.add)
                kb.tt(S32, S32, gC[:, :, ch:ch + 1].bc([128, 2, 128]), ALU.mult, eng="pool")
                kb.copy(Sb, S32, eng="pool")
            kb.dma(SV("oA", s, dirn, oA_s[s, dirn, :, :, t0:t0 + TB]), oblk, q="act")
            if dirn == 0:
                kb.dma(SV("gv", s, 0, gv_s[s, :, gch, :]), vblk.re("p n a v -> p n (a v)"), q="act")


    R0 = 1568
    wsm = kb.sb("wsm_st", [128, 3, 512], F32)
    kb.dma(wsm[0:64, 0, :], DV(w2_d[0]), q="sp")
    kb.dma(wsm[0:64, 1, :], DV(w2_d[1]), q="sp")
    kb.dma(wsm[64:128, 2, :], DV(a2_d), q="sp")
    w2b = kb.sb("w2b", [128, 2, 512], BF16)
    a2b = kb.sb("a2b", [128, 512], BF16)
    kb.copy(w2b[0:64, :, :], wsm[0:64, 0:2, :], eng="pool")
    kb.copy(a2b[64:128, :], wsm[64:128, 2, :], eng="pool")
    muc = kb.sb("muc", [128, 14], F32)
    kb.tt(muc, pv("mu_p"), pv("mu_n"), ALU.add)
    kb.ts(muc, muc, -1.0, 1.0, ALU.mult, ALU.add)
    omk = kb.sb("omk", [128, 4], F32)
    kb.ts(omk, pv("k_a"), -1.0, 1.0, ALU.mult, ALU.add)
    H32 = kb.sb("H32", [128, 4, 64], F32)
    Hb = kb.sb("Hb", [128, 4, 64], BF16)
    LW = 0.6065306597126334

    def rwkv_sweep(s, dirn, preW=None):
        kb.memset(H32, 0.0)
        kb.memset(Hb, 0.0)
        if dirn == 0:
            m1 = masks[:, 0:2, :]
            mN = masks[:, 2:3, :]
        else:
            m1 = masks[:, 2:4, :]
            mN = masks[:, 0:1, :]
        eye = masks[:, 4:5, :]
        w0n = "w0_f" if dirn == 0 else "w0_b"
        blocks = range(NB) if dirn == 0 else range(NB - 1, -1, -1)
        for b in blocks:
            t0 = b * TB
            psh = kb.banks[7]
            vT = kb.rot("r_vT", [128, 4, TB], BF16, n=2)
            r_f = kb.rot("r_rf", [128, 4, TB], F32, n=2)
            k_f = kb.rot("r_kf", [128, 4, TB], F32, n=2)
            wa_f = kb.rot("r_wa", [128, TB], F32, n=2)

            def proj_mix(Wt, wc, cc, dst):
                ps = kb.psum()
                for kc in range(8):
                    kb.mm(ps[:, 0:TB], Wt[:, kc, wc:wc + 128], hT[:, kc, 1 + t0:1 + t0 + TB], start=(kc == 0), stop=(kc == 7))
                if PM_MODE >= 1:
                    for kc in range(8):
                        kb.mm(psh[:, 2 * cc:2 * cc + 2], Wt[:, kc, wc:wc + 128], hT[:, kc, t0:t0 + TB + 2:TB + 1], start=(kc == 0), stop=(kc == 7))
                Psb = kb.rot("r_Psb", [128, TB + 2], F32, n=2)
                kb.copy(Psb[:, 1:TB + 1], ps[:, 0:TB], eng="act")
                if PM_MODE >= 1:
                    kb.copy(Psb[:, 0:TB + 2:TB + 1], psh[:, 2 * cc:2 * cc + 2], eng="act")
                if PM_MODE >= 2:
                    s1 = kb.rot("r_s1", [128, TB], F32, n=2)
                    kb.ts(s1, ps[:, 0:TB], muc[:, cc:cc + 1], None, ALU.mult)
                    kb.stt(s1, Psb[:, 0:TB], pv("mu_p", cc), s1, ALU.mult, ALU.add)
                    kb.stt(dst, Psb[:, 2:TB + 2], pv("mu_n", cc), s1, ALU.mult, ALU.add)

            tsl_ = slice(t0, t0 + TB)
            if dirn == 0:
                for g_, dstt in ((0, r_f), (1, k_f), (2, vT)):
                    Wt = preW if (g_ == 0 and b == 0 and preW is not None) else loadw(R0 + g_ * 512, 512)
                    for pc in range(4):
                        proj_mix(Wt, pc * 128, g_ * 4 + pc, dstt[:, pc, :])
                Wt = loadw(R0 + 1536, 256)
                proj_mix(Wt, 0, 12, wa_f)
                kb.dma(SV("rf", s, 0, rf_s[s, :, :, tsl_]), r_f, q="act")
                kb.dma(SV("kf", s, 0, kf_s[s, :, :, tsl_]), k_f, q="act")
                kb.dma(SV("vt", s, 0, vt_s[s, :, :, tsl_]), vT, q="act")
                kb.dma(SV("wa", s, 0, wa_s[s, :, tsl_]), wa_f, q="act")
            else:
                kb.dma(r_f, SV("rf", s, 0, rf_s[s, :, :, tsl_]), q="sp")
                kb.dma(wa_f, SV("wa", s, 0, wa_s[s, :, tsl_]), q="sp")
            if dirn == 0:
                gl_f = kb.rot("r_gl", [128, TB], F32, n=1)
                proj_mix(Wt, 128, 13, gl_f)
                sglb = kb.rot("r_sglb", [128, TB], BF16, n=1)
                kb.act(sglb, gl_f, AF.Sigmoid)
                kb.dma(SV("sgl", s, 0, sgl_s[s, :, t0:t0 + TB]), sglb, q="act")
                bonblk = kb.rot("r_bon", [128, 4, TB], F32, n=1)
            if RW_STOP <= 1:
                continue
            twal = kb.rot("r_twal", [128, TB], BF16, n=1)
            kb.act(twal[0:64, :], wa_f[0:64, :], AF.Tanh)
            kb.copy(twal[64:128, :], wa_f[64:128, :], eng="pool")
            AR = kb.rot("r_AR", [128, 4, NCH, 128], BF16, n=1)
            ktT = kb.rot("r_ktT", [128, 4, TB], BF16, n=1)
            btT = kb.rot("r_btT", [128, 4, TB], BF16, n=1)
            gC = kb.rot("r_gC", [128, 4, NCH], F32, n=1)
            T4 = lambda tag, dt=F32: kb.rot("r_q_" + tag, [128, 4, TB], dt, n=1)
            bcp = lambda name: pv(name).re("p (a o) -> p a o", o=1).bc([128, 4, TB])
            sgA, aA = T4("sg"), T4("a")
            for pc in range(4):
                ps = kb.psum()
                kb.mm(ps[:, 0:TB], w2b[0:64, dirn, pc * 128:pc * 128 + 128], twal[0:64, :])
                kb.act(sgA[:, pc, :], ps[:, 0:TB], AF.Sigmoid, bias=pv(w0n, pc))
            if dirn == 0:
                for pc in range(4):
                    ps = kb.psum()
                    kb.mm(ps[:, 0:TB], a2b[64:128, pc * 128:pc * 128 + 128], twal[64:128, :])
                    kb.act(aA[:, pc, :], ps[:, 0:TB], AF.Sigmoid, bias=pv("a0", pc))
                kkA = T4("kk")
                kb.tt(kkA, k_f, bcp("k_k"), ALU.mult)
                sqb = T4("sqb", BF16)
                kb.tt(sqb, kkA, kkA, ALU.mult, eng="pool")
                rnA = T4("rn")
                for pc in range(4):
                    ps = kb.psum()
                    kb.mm(ps[:, 0:TB], bones, sqb[:, pc, :])
                    kb.act(rnA[:, pc, :], ps[:, 0:TB], AF.Ln, bias=1e-24)
                kb.act(rnA, rnA, AF.Exp, scale=-0.5)
                kb.tt(kkA, kkA, rnA, ALU.mult, eng="pool")
                taA = rnA
                kb.tt(taA, aA, bcp("k_a"), ALU.mult, eng="pool")
                kb.tt(taA, taA, omk.re("p (a o) -> p a o", o=1).bc([128, 4, TB]), ALU.add, eng="pool")
                kmodA = T4("kmod")
                kb.tt(kmodA, k_f, taA, ALU.mult)
                betaA = T4("beta")
                kb.tt(betaA, kkA, aA, ALU.mult, eng="pool")
                kb.dma(SV("kk", s, 0, kk_s[s, :, :, tsl_]), kkA, q="act")
                kb.dma(SV("km", s, 0, km_s[s, :, :, tsl_]), kmodA, q="act")
                kb.dma(SV("be", s, 0, be_s[s, :, :, tsl_]), betaA, q="act")
            else:
                kkA, kmodA, betaA = T4("kk"), T4("kmod"), T4("beta")
                kb.dma(kkA, SV("kk", s, 0, kk_s[s, :, :, tsl_]), q="sp")
                kb.dma(kmodA, SV("km", s, 0, km_s[s, :, :, tsl_]), q="sp")
                kb.dma(betaA, SV("be", s, 0, be_s[s, :, :, tsl_]), q="sp")
            if dirn == 0:
                rkb = T4("rkb", BF16)
                kb.tt(taA, r_f, bcp("r_k"), ALU.mult, eng="pool")
                kb.tt(rkb, taA, kmodA, ALU.mult)
                for pc in range(4):
                    ps = kb.psum()
                    kb.mm(ps[:, 0:TB], bones, rkb[:, pc, :])
                    kb.tt(bonblk[:, pc, :], ps[:, 0:TB], vT[:, pc, :], ALU.mult)
            csA = T4("cs")
            fl = lambda t_: t_.re("p a t -> p (a t)")
            kb.scan(fl(csA), cmask[:, 0:4 * TB], fl(sgA))
            if dirn == 1:
                c3 = fl(csA).re("p (n c) -> p n c", c=CH)
                l3 = fl(sgA).re("p (n c) -> p n c", c=CH)
                tot = kb.rot("r_tot", [128, 4 * NCH, 1], F32, n=1)
                kb.copy(tot, c3[:, :, CH - 1:CH], eng="pool")
                kb.tt(c3, l3, c3, ALU.subtract)
                kb.tt(c3, c3, tot.bc([128, 4 * NCH, CH]), ALU.add)
            csmA = T4("csm")
            kb.tt(csmA, csA, sgA, ALU.subtract, eng="pool")
            GA, GiA, GpA = T4("G"), csA, csmA
            kb.act(GA, csA, AF.Exp, scale=-LW)
            kb.act(GiA, csA, AF.Exp, scale=LW)
            kb.act(GpA, csmA, AF.Exp, scale=-LW)
            v4 = lambda t_: t_.re("p a (n c) -> p a n c", c=CH)
            kb.tt(AR[:, :, :, 64:128], v4(r_f), v4(GA), ALU.mult)
            kb.stt(AR[:, :, :, 0:64], v4(kkA), -1.0, v4(GpA), ALU.mult, ALU.mult)
            kb.tt(ktT, kmodA, GiA, ALU.mult)
            kb.tt(btT, betaA, GiA, ALU.mult, eng="pool")
            edge = CH - 1 if dirn == 0 else 0
            kb.copy(gC.re("p a (n o) -> p a n o", o=1), v4(GA)[:, :, :, edge:edge + 1], eng="pool")
            yblk = kb.rot("r_yblk", [128, 4, TB], F32, n=1)
            chunks = range(NCH) if dirn == 0 else range(NCH - 1, -1, -1)
            H8 = [(h, slice((h % 2) * 64, (h % 2) * 64 + 64), h // 2) for h in range(8)]
            chl = list(chunks)
            ccs = {ch: slice(ch * CH, (ch + 1) * CH) for ch in chl}
            L = {ch: {} for ch in chl}
            m1b = m1.re("p a c -> p (a c)").re("p (o x) -> p o x", o=1).bc([128, 4, 128])
            for ch in chl:
                cc = ccs[ch]
                psT = kb.psum().bitcast(BF16)
                srcs = (lambda p_: ktT[:, p_, cc], lambda p_: AR[:, p_, ch, 0:64], lambda p_: btT[:, p_, cc], lambda p_: vT[:, p_, cc])
                ntr = 4 if dirn == 0 else 3
                for ti, sf in enumerate(srcs[:ntr]):
                    for h, hp, pr in H8:
                        kb.tr(psT[hp, (ti * 4 + pr) * 64:(ti * 4 + pr) * 64 + 64], sf(pr)[hp, :], ident[hp, hp])
                tm = kb.rot("r_tm%d" % ch, [128, 4, 4, 64], BF16, n=1)
                kb.copy(tm[:, 0:ntr], psT[:, 0:ntr * 256].re("p (a b c) -> p a b c", a=ntr, b=4), eng="act")
                gci = b * NCH + ch
                if dirn == 0:
                    kb.dma(SV("rv", s, 0, rv_s[s, :, gci, :]), tm[:, 3].re("p b c -> p (b c)"), q="act")
                else:
                    kb.dma(tm[:, 3].re("p b c -> p (b c)"), SV("rv", s, 0, rv_s[s, :, gci, :]), q="sp")
                L[ch]["tm"] = tm
            for ch in chl:
                cc = ccs[ch]
                ps1 = kb.psum()
                ps2 = kb.psum()
                ps3 = kb.psum()
                for h, hp, pr in H8:
                    kb.mm(ps1[hp, pr * 128:pr * 128 + 128], btT[hp, pr, cc], AR[hp, pr, ch, :])
                for h, hp, pr in H8:
                    kb.mm(ps2[hp, pr * 128:pr * 128 + 128], ktT[hp, pr, cc], AR[hp, pr, ch, :])
                for h, hp, pr in H8:
                    kb.mm(ps3[hp, pr * 64:pr * 64 + 64], AR[hp, pr, ch, 0:64], btT[hp, pr, cc])
                NA = kb.rot("r_NA%d" % ch, [128, 4, 128], BF16, n=1)
                MA = kb.rot("r_MA%d" % ch, [128, 4, 128], BF16, n=1)
                Nn = kb.rot("r_Nn%d" % ch, [128, 4, 64], BF16, n=1)
                kb.tt(NA, ps1[:, 0:512].re("p (a c) -> p a c", a=4), m1b, ALU.mult)
                kb.tt(MA, ps2[:, 0:512].re("p (a c) -> p a c", a=4), m1b, ALU.mult)
                kb.tt(Nn, ps3[:, 0:256].re("p (a c) -> p a c", a=4), mN.bc([128, 4, 64]), ALU.mult)
                L[ch].update(NA=NA, MA=MA, Nn=Nn)
            for ch in chl:
                psz = kb.psum()
                MA, vTM = L[ch]["MA"], L[ch]["tm"][:, 3]
                for h, hp, pr in H8:
                    kb.mm(psz[hp, pr * 64:pr * 64 + 64], MA[hp, pr, 0:64], vTM[hp, pr, :])
                Z0 = kb.rot("r_Z0%d" % ch, [128, 4, 64], BF16, n=1)
                kb.copy(Z0, psz[:, 0:256].re("p (a c) -> p a c", a=4), eng="act")
                Y = kb.rot("r_Y%d" % ch, [128, 4, 64], BF16, n=2)
                kb.tt(Y, L[ch]["NA"][:, :, 0:64], eye.bc([128, 4, 64]), ALU.add, eng="pool")
                L[ch].update(Z0=Z0, Y=Y, P=(lambda N_: (lambda p_: N_[:, p_, :]))(L[ch]["Nn"]),
                             PT=(lambda N_: (lambda p_: N_[:, p_, 0:64]))(L[ch]["NA"]))
            for j in range(1, 6):
                for ch in chl:
                    Pj, PTj = L[ch]["P"], L[ch]["PT"]
                    psq = kb.psum()
                    for h, hp, pr in H8:
                        kb.mm(psq[hp, (pr * 2) * 64:(pr * 2) * 64 + 64], PTj(pr)[hp, :], Pj(pr)[hp, :])
                        if j < 5:
                            kb.mm(psq[hp, (pr * 2 + 1) * 64:(pr * 2 + 1) * 64 + 64], Pj(pr)[hp, :], PTj(pr)[hp, :])
                    PP = kb.rot("r_PP%d" % ch, [128, 4, 2, 64], BF16, n=2)
                    if j < 5:
                        kb.copy(PP, psq[:, 0:512].re("p (a b c) -> p a b c", a=4, b=2), eng="act")
                    else:
                        kb.copy(PP[:, :, 0, :], psq[:, 0:512].re("p (a b c) -> p a b c", a=4, b=2)[:, :, 0, :], eng="act")
                    L[ch]["P"] = (lambda PP_: (lambda p_: PP_[:, p_, 0, :]))(PP)
                    L[ch]["PT"] = (lambda PP_: (lambda p_: PP_[:, p_, 1, :]))(PP)
                for ch in chl:
                    Pj, Y = L[ch]["P"], L[ch]["Y"]
                    psy = kb.psum()
                    for h, hp, pr in H8:
                        kb.mm(psy[hp, pr * 64:pr * 64 + 64], Pj(pr)[hp, :], Y[hp, pr, :])
                    Yn = kb.rot("r_Y%d" % ch, [128, 4, 64], BF16, n=2)
                    kb.tt(Yn, psy[:, 0:256].re("p (a c) -> p a c", a=4), Y, ALU.add)
                    L[ch]["Y"] = Yn
            for ch in chl:
                TTt, aTM = L[ch]["Y"], L[ch]["tm"][:, 1]
                psw = kb.psum()
                for h, hp, pr in H8:
                    kb.mm(psw[hp, pr * 64:pr * 64 + 64], aTM[hp, pr, :], TTt[hp, pr, :])
                WTg = kb.rot("r_WT%d" % ch, [128, 4, 64], BF16, n=1)
                kb.copy(WTg, psw[:, 0:256].re("p (a c) -> p a c", a=4), eng="act")
                L[ch]["WT"] = WTg
            for ch in chl:
                cc = ccs[ch]
                tm, NA, MA, Z0, TTt, WTg = L[ch]["tm"], L[ch]["NA"], L[ch]["MA"], L[ch]["Z0"], L[ch]["Y"], L[ch]["WT"]
                kTM, bTM, vTM = tm[:, 0], tm[:, 2], tm[:, 3]
                psu = kb.psum()
                for h, hp, pr in H8:
                    kb.mm(psu[hp, pr * 64:pr * 64 + 64], TTt[hp, pr, :], Z0[hp, pr, :], start=True, stop=False)
                    kb.mm(psu[hp, pr * 64:pr * 64 + 64], WTg[hp, pr, :], Hb[hp, pr, :], start=False, stop=True)
                Ug = kb.rot("r_U", [128, 4, 64], BF16, n=2)
                kb.copy(Ug, psu[:, 0:256].re("p (a c) -> p a c", a=4), eng="act")
                psH = kb.psum()
                for h, hp, pr in H8:
                    o_ = psH[hp, pr * 64:pr * 64 + 64]
                    kb.mm(o_, kTM[hp, pr, :], vTM[hp, pr, :], start=True, stop=False)
                    kb.mm(o_, bTM[hp, pr, :], Ug[hp, pr, :], start=False, stop=True)
                kb.tt(H32, psH[:, 0:256].re("p (a c) -> p a c", a=4), H32, ALU.add)
                psY = kb.psum()
                for h, hp, pr in H8:
                    o_ = psY[hp, pr * 64:pr * 64 + 64]
                    kb.mm(o_, Hb[hp, pr, :], AR[hp, pr, ch, 64:128], start=True, stop=False)
                    kb.mm(o_, Ug[hp, pr, :], NA[hp, pr, 64:128], start=False, stop=False)
                    kb.mm(o_, vTM[hp, pr, :], MA[hp, pr, 64:128], start=False, stop=True)
                kb.tt(Hb, H32, gC[:, :, ch:ch + 1].bc([128, 4, 64]), ALU.mult, eng="pool")
                kb.tt(H32, H32, gC[:, :, ch:ch + 1].bc([128, 4, 64]), ALU.mult, eng="pool")
                kb.copy(yblk[:, :, cc], psY[:, 0:256].re("p (a c) -> p a c", a=4), eng="act")
            kb.dma(SV("yB", s, dirn, yB_s[s, dirn, :, :, t0:t0 + TB]), yblk, q="act")
            if dirn == 0:
                kb.dma(SV("bon", s, 0, bon_s[s, :, :, t0:t0 + TB]), bonblk, q="act")

    def compute_hT(s):
        xs = xT[s].rearrange("(kc p) t -> p kc t", p=128)
        kb.memset(hT[:, :, 0:1], 0.0)
        kb.memset(hT[:, :, T + 1:T + 2], 0.0)
        for b in range(NB):
            t0 = b * TB
            xblk = kb.rot("xblk", [128, 8, TB], F32, n=2)
            kb.dma(xblk, DV(xs[:, :, t0:t0 + TB]), q="sp")
            sq = kb.rot("sq", [128, 8, TB], BF16, n=1)
            rstd = kb.rot("rstd", [128, TB], F32, n=2)
            kb.act(sq, xblk, AF.Square)
            ps = kb.psum()
            for kc in range(8):
                kb.mm(ps[:, 0:TB], ones, sq[:, kc, :], start=(kc == 0), stop=(kc == 7))
            rsqrt(rstd, ps[:, 0:TB], float(D) * 1e-6)
            for kc in range(8):
                kb.stt(hT[:, kc, 1 + t0:1 + t0 + TB], xblk[:, kc, :], 32.0, rstd, ALU.mult, ALU.mult)

    def final_phase(s, preW=None):
        xs = xT[s].rearrange("(kc p) t -> p kc t", p=128)
        Wgp = kb.rot("f_Wgp", [128, 4, D], BF16, n=1)
        Wrp = kb.rot("f_Wrp", [128, 4, D], BF16, n=1)
        Wo = kb.rot("f_Wo", [128, 8, D], BF16, n=1)
        kb.dma(Wgp, DV(glap_s[:, :, :]), q="sp")
        kb.dma(Wrp, DV(rwp_s[:, :, :]), q="sp")
        kb.dma(Wo, DV(wout_s[:, :, :]), q="sp")
        x1v = x1_s[s].rearrange("(kc p) t -> p kc t", p=128)
        for b in range(NB):
            t0 = b * TB
            hb = lambda kc: hT[:, kc, 1 + t0:1 + t0 + TB]
            tsl = slice(t0, t0 + TB)
            of_ = kb.rot("f_a", [128, 4, TB], F32, n=2)
            ob_ = kb.rot("f_b", [128, 4, TB], F32, n=2)
            kb.dma(of_, SV("oA", s, 0, oA_s[s, 0, :, :, tsl]), q="sp")
            kb.dma(ob_, SV("oA", s, 1, oA_s[s, 1, :, :, tsl]), q="sp")
            yf = kb.rot("f_a", [128, 4, TB], F32, n=2)
            yb_ = kb.rot("f_b", [128, 4, TB], F32, n=2)
            kb.dma(yf, SV("yB", s, 0, yB_s[s, 0, :, :, tsl]), q="sp")
            kb.dma(yb_, SV("yB", s, 1, yB_s[s, 1, :, :, tsl]), q="sp")
            bon = kb.rot("f_bon", [128, 4, TB], F32, n=2)
            kb.dma(bon, SV("bon", s, 0, bon_s[s, :, :, tsl]), q="sp")
            sglb = kb.rot("f_sgl", [128, TB], BF16, n=2)
            kb.dma(sglb, SV("sgl", s, 0, sgl_s[s, :, tsl]), q="sp")
            kb.tt(of_, of_, ob_, ALU.add, eng="pool")
            sqg = kb.rot("f_sq", [128, 4, TB], BF16, n=1)
            kb.tt(sqg, of_, of_, ALU.mult, eng="pool")
            Wog = preW if (b == 0 and preW is not None) else loadw(1024, 512)
            ofin = kb.rot("f_ofin", [128, 4, TB], BF16, n=1)
            ons = []
            for h in range(4):
                ps = kb.psum()
                kb.mm(ps[:, 0:TB], ones, sqg[:, h, :])
                rs = kb.rot("f_rs", [128, TB], F32, n=2)
                rsqrt(rs, ps[:, 0:TB], 128.0 * 1e-5)
                on = kb.rot("f_on", [128, TB], F32, n=4)
                kb.stt(on, of_[:, h, :], pv("gng", h), rs, ALU.mult, ALU.mult)
                ons.append(on)
            for h in range(4):
                on = ons[h]
                ps2 = kb.psum()
                for kc in range(8):
                    kb.mm(ps2[:, 0:TB], Wog[:, kc, h * 128:h * 128 + 128], hb(kc), start=(kc == 0), stop=(kc == 7))
                sg = kb.rot("f_sg", [128, TB], F32, n=2)
                kb.act(sg, ps2[:, 0:TB], AF.Sigmoid)
                kb.tt(sg, ps2[:, 0:TB], sg, ALU.mult)
                kb.stt(ofin[:, h, :], on, float(np.sqrt(128.0)), sg, ALU.mult, ALU.mult)
            kb.tt(yf, yf, yb_, ALU.add, eng="pool")
            y16 = kb.rot("f_y16", [128, 4, TB], BF16, n=1)
            kb.copy(y16, yf, eng="pool")
            ysq = kb.rot("f_sq", [128, 4, TB], BF16, n=1)
            kb.tt(ysq, yf, yf, ALU.mult, eng="pool")
            orw = kb.rot("f_orw", [128, 4, TB], BF16, n=1)
            for pc in range(4):
                psm = kb.psum()
                kb.mm(psm[:, 0:TB], bones, y16[:, pc, :])
                psq = kb.psum()
                kb.mm(psq[:, 0:TB], bones, ysq[:, pc, :])
                mean = kb.rot("f_mean", [128, TB], F32, n=2)
                kb.act(mean, psm[:, 0:TB], AF.Copy, scale=1.0 / 64.0)
                msq = kb.rot("f_msq", [128, TB], F32, n=2)
                kb.tt(msq, mean, mean, ALU.mult, eng="pool")
                var = kb.rot("f_var", [128, TB], F32, n=2)
                kb.stt(var, psq[:, 0:TB], 1.0 / 64.0, msq, ALU.mult, ALU.subtract)
                rsqrt(var, var, 64.0 * 1e-5)
                yc = kb.rot("f_yc", [128, TB], F32, n=2)
                kb.tt(yc, yf[:, pc, :], mean, ALU.subtract, eng="pool")
                kb.tt(yc, yc, var, ALU.mult, eng="pool")
                kb.ts(yc, yc, pv("ln_w", pc), pv("ln_b", pc), ALU.mult, ALU.add)
                kb.tt(yc, yc, bon[:, pc, :], ALU.add, eng="pool")
                psg = kb.psum()
                kb.mm(psg[:, 0:TB], g2b[:, pc * 128:pc * 128 + 128], sglb)
                kb.tt(orw[:, pc, :], psg[:, 0:TB], yc, ALU.mult)
            mrg = kb.rot("f_mrg", [128, 8, TB], F32, n=1)
            mrgb = kb.rot("f_mrgb", [128, 8, TB], BF16, n=1)
            for bi, (pview, feat, gc0) in enumerate(((Wgp, ofin, 3360), (Wrp, orw, 3360 + 1024))):
                for hf in range(2):
                    Wg = loadw(gc0 + hf * 512, 512)
                    for j in range(4):
                        oc = hf * 4 + j
                        psA = kb.psum()
                        for kc in range(4):
                            kb.mm(psA[:, 0:TB], pview[:, kc, oc * 128:oc * 128 + 128], feat[:, kc, :], start=(kc == 0), stop=(kc == 3))
                        psG = kb.psum()
                        for kc in range(8):
                            kb.mm(psG[:, 0:TB], Wg[:, kc, j * 128:j * 128 + 128], hb(kc), start=(kc == 0), stop=(kc == 7))
                        sgt = kb.rot("f_sg", [128, TB], F32, n=2)
                        kb.act(sgt, psG[:, 0:TB], AF.Sigmoid)
                        if bi == 0:
                            kb.tt(mrg[:, oc, :], psA[:, 0:TB], sgt, ALU.mult)
                        else:
                            kb.tt(sgt, psA[:, 0:TB], sgt, ALU.mult)
                            kb.tt(mrgb[:, oc, :], mrg[:, oc, :], sgt, ALU.add, eng="pool")
            xblk = kb.rot("xblk", [128, 8, TB], F32, n=2)
            kb.dma(xblk, DV(xs[:, :, tsl]), q="sp")
            for hf in range(2):
                for j in range(4):
                    oc = hf * 4 + j
                    ps = kb.psum()
                    for kc in range(8):
                        kb.mm(ps[:, 0:TB], Wo[:, kc, oc * 128:oc * 128 + 128], mrgb[:, kc, :], start=(kc == 0), stop=(kc == 7))
                    kb.tt(xblk[:, oc, :], ps[:, 0:TB], xblk[:, oc, :], ALU.add)
            kb.dma(SV("x1", s, 0, x1v[:, :, tsl]), xblk, q="act")

    alias_tiles = {}

    def ffn_phase(s):
        x1v = x1_s[s].rearrange("(kc p) t -> p kc t", p=128)
        ov = outT[s].rearrange("(kc p) t -> p kc t", p=128)
        hid = kb.rot("n_hid", [128, 22, TBF], BF16, n=1)
        kb.memset(hid[:, 21, :], 0.0)
        psh = kb.banks[7]
        if hT_bytes >= 4 * 8192:
            if "ffn_w" not in alias_tiles:
                alias_tiles["ffn_w"] = [kb.view_at("n_w%d" % i, hT_off + i * 8192, [128, 8, 512], BF16) for i in range(4)]
            fw = alias_tiles["ffn_w"]
        else:
            fw = [kb.rot("n_wsm", [128, 8, 512], BF16, n=4) for _ in range(4)]
        fcnt = [0]

        def floadw(c0, ncols):
            wb = fw[fcnt[0] % 4]
            fcnt[0] += 1
            kb.dma(wb[:, :, 0:ncols], DV(fup_s[:, :, c0:c0 + ncols]), q="sp")
            return wb

        def load_x1h(bb):
            tt0 = bb * TBF
            xh = kb.rot("n_x1h", [128, 8, TBF + 2], F32, n=2)
            kb.memset(xh[:, :, 0:1], 0.0)
            kb.memset(xh[:, :, TBF + 1:TBF + 2], 0.0)
            lo, hi = max(tt0 - 1, 0), min(tt0 + TBF + 1, T)
            kb.dma(xh[:, :, lo - (tt0 - 1):hi - (tt0 - 1)], SV("x1", s, 0, x1v[:, :, lo:hi]), q="sp")
            return xh

        nxt = load_x1h(0)
        for b in range(NBF):
            t0 = b * TBF
            x1h = nxt
            sq = kb.rot("n_sq", [128, 8, TBF + 2], BF16, n=1)
            kb.act(sq, x1h, AF.Square)
            ps = kb.psum()
            for kc in range(8):
                kb.mm(ps[:, 0:TBF], ones, sq[:, kc, 1:TBF + 1], start=(kc == 0), stop=(kc == 7))
            for kc in range(8):
                kb.mm(psh[:, 0:2], ones, sq[:, kc, 0:TBF + 2:TBF + 1], start=(kc == 0), stop=(kc == 7))
            rstd = kb.rot("n_rstd", [128, TBF + 2], F32, n=1)
            rsqrt(rstd[:, 1:TBF + 1], ps[:, 0:TBF], float(D) * 1e-6)
            rsqrt(rstd[:, 0:TBF + 2:TBF + 1], psh[:, 0:2], float(D) * 1e-6)
            h2T = kb.rot("n_h2T", [128, 8, TBF + 2], BF16, n=1)
            for kc in range(8):
                kb.stt(h2T[:, kc, :], x1h[:, kc, :], 32.0, rstd, ALU.mult, ALU.mult)
            if b + 1 < NBF:
                nxt = load_x1h(b + 1)
            pend = [None]
            for j0 in range(0, 22, 4):
                jn = min(4, 22 - j0)
                ncols = sum(128 if j < 21 else 64 for j in range(j0, j0 + jn))
                Wg4 = floadw(j0 * 128, ncols)
                Wv4 = floadw(DFF + j0 * 128, ncols)
                for jj in range(jn):
                    j = j0 + jj
                    nj = 128 if j < 21 else 64
                    cres = []
                    for gv, Wt in ((0, Wg4), (1, Wv4)):
                        cwn, cbn = ("cwg", "cbg") if gv == 0 else ("cwv", "cbv")
                        ps = kb.psum()
                        for kc in range(8):
                            kb.mm(ps[0:nj, 0:TBF], Wt[:, kc, jj * 128:jj * 128 + nj], h2T[:, kc, 1:TBF + 1], start=(kc == 0), stop=(kc == 7))
                        hc = slice(4 + 2 * gv, 6 + 2 * gv)
                        for kc in range(8):
                            kb.mm(psh[0:nj, hc], Wt[:, kc, jj * 128:jj * 128 + nj], h2T[:, kc, 0:TBF + 2:TBF + 1], start=(kc == 0), stop=(kc == 7))
                        U = kb.rot("n_U", [128, TBF + 2], F32, n=2)
                        kb.copy(U[0:nj, 1:TBF + 1], ps[0:nj, 0:TBF], eng="act")
                        kb.copy(U[0:nj, 0:TBF + 2:TBF + 1], psh[0:nj, hc], eng="act")
                        c = kb.rot("n_c%d" % gv, [128, TBF], F32, n=3)
                        kb.ts(c[0:nj, :], U[0:nj, 1:TBF + 1], pv(cwn, 22 + j)[0:nj, :], pv(cbn, j)[0:nj, :], ALU.mult, ALU.add, eng="pool")
                        kb.stt(c[0:nj, :], U[0:nj, 0:TBF], pv(cwn, j)[0:nj, :], c[0:nj, :], ALU.mult, ALU.add)
                        kb.stt(c[0:nj, :], U[0:nj, 2:TBF + 2], pv(cwn, 44 + j)[0:nj, :], c[0:nj, :], ALU.mult, ALU.add)
                        cres.append(c)
                    if pend[0] is not None:
                        pend[0]()

                    def fin(cg=cres[0], cv=cres[1], nj=nj, j=j):
                        kb.act(cg[0:nj, :], cg[0:nj, :], AF.Silu)
                        kb.tt(hid[0:nj, j, :], cg[0:nj, :], cv[0:nj, :], ALU.mult, eng="pool")
                    pend[0] = fin
            pend[0]()
            pend[0] = None
            x2 = x1h[:, :, 1:TBF + 1]
            for oc in range(8):
                if oc % 2 == 0:
                    Wdh = kb.rot("n_wd", [128, 22, 256], BF16, n=2)
                    kb.dma(Wdh, DV(wd_s[:, :, (oc // 2) * 256:(oc // 2) * 256 + 256]), q="sp")
                ps = kb.psum()
                for j in range(22):
                    kb.mm(ps[:, 0:TBF], Wdh[:, j, (oc % 2) * 128:(oc % 2) * 128 + 128], hid[:, j, :], start=(j == 0), stop=(j == 21))
                kb.tt(x2[:, oc, :], ps[:, 0:TBF], x2[:, oc, :], ALU.add)
            sq2 = kb.rot("n_sq", [128, 8, TBF + 2], BF16, n=1)
            kb.act(sq2[:, :, 0:TBF], x2, AF.Square)
            ps = kb.psum()
            for kc in range(8):
                kb.mm(ps[:, 0:TBF], ones, sq2[:, kc, 0:TBF], start=(kc == 0), stop=(kc == 7))
            rstd2 = kb.rot("n_rstd", [128, TBF + 2], F32, n=1)
            rsqrt(rstd2[:, 0:TBF], ps[:, 0:TBF], float(D) * 1e-6)
            for oc in range(8):
                kb.stt(x2[:, oc, :], x2[:, oc, :], gf32[:, oc:oc + 1], rstd2[:, 0:TBF], ALU.mult, ALU.mult)
            kb.dma(DV(ov[:, :, t0:t0 + TBF]), x2, q="act")

    kb.set_phase("prep")
    precast(w_in_v, 8, NPROJ, Win_s, "g1")
    precast(glap_v, 4, D, glap_s, None)
    precast(rwp_v, 4, D, rwp_s, None)
    precast(wout_v, 8, D, wout_s, None)
    precast(fup_v, 8, 2 * DFF, fup_s, "g2n")
    load_Wd()
    for s in range(NSEQ):
        kb.set_phase("hT")
        pre = loadw(0, 512)
        compute_hT(s)
        gla_sweep(s, 0, pre)
        gla_sweep(s, 1)
        pre = loadw(R0, 512)
        kb.set_phase("rwkv")
        rwkv_sweep(s, 0, pre)
        rwkv_sweep(s, 1)
        pre = loadw(1024, 512)
        kb.set_phase("final")
        final_phase(s, pre)
        kb.set_phase("ffn")
        ffn_phase(s)
    kb.P.emit()
    return nc


def pack_params(inp):
    pvm = np.zeros((128, PV_W), np.float32)

    def put(name, vec):
        o, w = PV_OFF[name]
        pvm[:, o:o + w] = _fm(vec, w)

    put("g1", inp["norm1_g"][0])
    put("g2n", inp["norm2_g"][0])
    put("gf", inp["norm_f_g"])
    put("ba_f", inp["gla_ba_f"][0])
    put("ba_b", inp["gla_ba_b"][0])
    put("gng", inp["gla_norm_g"][0])
    put("mu_p", inp["rwkv_mu_prev"][0])
    put("mu_n", inp["rwkv_mu_next"][0])
    put("w0_f", inp["rwkv_w0_f"][0])
    put("w0_b", inp["rwkv_w0_b"][0])
    put("a0", inp["rwkv_a0"][0])
    put("k_k", inp["rwkv_k_k"][0])
    put("k_a", inp["rwkv_k_a"][0])
    put("r_k", np.asarray(inp["rwkv_r_k"][0]).reshape(-1))
    put("ln_w", inp["rwkv_ln_w"][0])
    put("ln_b", inp["rwkv_ln_b"][0])
    cw = np.asarray(inp["ffn_conv_w"][0], np.float32)
    cb = np.asarray(inp["ffn_conv_b"][0], np.float32)

    def chunked(v):
        pad = np.zeros(22 * 128, np.float32)
        pad[:DFF] = v
        return pad.reshape(22, 128).T

    for nm, off in (("cwg", 0), ("cwv", DFF)):
        o, w = PV_OFF[nm]
        for tap in range(3):
            pvm[:, o + tap * 22:o + tap * 22 + 22] = chunked(cw[tap, off:off + DFF])
    for nm, off in (("cbg", 0), ("cbv", DFF)):
        o, w = PV_OFF[nm]
        pvm[:, o:o + 22] = chunked(cb[off:off + DFF])
    return pvm


_CACHE = {}


def run(inputs, n_cores, nseq, T):
    inp = {k: np.asarray(v) for k, v in inputs.items()}
    x = inp["x"]
    key = (n_cores, nseq, T)
    if key not in _CACHE:
        nc = bass.Bass("TRN2", target_bir_lowering=False)
        build(nc, nseq, T)
        _CACHE[key] = nc
    nc = _CACHE[key]
    shared = dict(make_consts())
    shared["pvec"] = pack_params(inp)
    f32c = lambda a: np.ascontiguousarray(np.asarray(a, np.float32))
    shared.update({
        "w_in": f32c(inp["w_in"][0]), "wa2_f": f32c(inp["gla_wa2_f"][0]), "wa2_b": f32c(inp["gla_wa2_b"][0]),
        "w2_f": f32c(inp["rwkv_w2_f"][0]), "w2_b": f32c(inp["rwkv_w2_b"][0]), "a2": f32c(inp["rwkv_a2"][0]),
        "g2": f32c(inp["rwkv_g2"][0]), "gla_proj": f32c(inp["gla_proj"][0]), "rwkv_proj": f32c(inp["rwkv_proj"][0]),
        "w_out": f32c(inp["w_out"][0]), "ffn_up": f32c(inp["ffn_up"][0]), "ffn_down": f32c(inp["ffn_down"][0]),
    })
    in_maps = []
    for c in range(n_cores):
        xs = x[c * nseq:(c + 1) * nseq]
        m = dict(shared)
        m["xT"] = np.ascontiguousarray(xs.transpose(0, 2, 1))
        in_maps.append(m)
    res = run_bass_kernel_spmd(nc, in_maps, core_ids=list(range(n_cores)))
    outs = [np.asarray(r["outT"]).transpose(0, 2, 1) for r in res.results]
    return np.ascontiguousarray(np.concatenate(outs, axis=0)).astype(np.float32)


def kernel(**inputs):
    x = np.asarray(inputs["x"])
    B, T, _ = x.shape
    return run(inputs, N_CORES, B // N_CORES, T)
```
